# Optimizing a Trainium2 kernel written in Bass

```python
import jax, jax.numpy as jnp
from jax import lax
import numpy as np

D_MODEL = 1024
BATCH = 2
SEQ = 8192
DEPTH = 1
DEC_BATCH = 16
DEC_SEQ = 32
PAST_LEN = 4096

CHUNK = 64
D_CONV = D_MODEL // 2
D_POOL = D_MODEL // 2
CONV_WIDTH = 31
CONV_HIST = CONV_WIDTH - 1
POOL_WINDOWS = (2, 4, 8, 16)
N_POOL_GROUPS = len(POOL_WINDOWS)
POOL_GROUP = D_POOL // N_POOL_GROUPS
POOL_HIST = max(POOL_WINDOWS) - 1
D_IN = 2 * D_CONV + D_POOL
D_FF = ((8 * D_MODEL // 3 + 127) // 128) * 128
ALPHA = (2.0 * DEPTH) ** 0.25
BETA = (8.0 * DEPTH) ** -0.25
LN_EPS = 1e-5

kernel_name = "streaming_conv_pool_hybrid_step"


def layer_norm(x, g, b):
    xf = x.astype(jnp.float32)
    mu = jnp.mean(xf, axis=-1, keepdims=True)
    var = jnp.mean(jnp.square(xf - mu), axis=-1, keepdims=True)
    y = (xf - mu) * lax.rsqrt(var + LN_EPS)
    return (y * g.astype(jnp.float32) + b.astype(jnp.float32)).astype(x.dtype)


def swiglu(x, w_g, w_u, w_d):
    return (jax.nn.silu(x @ w_g) * (x @ w_u)) @ w_d


def causal_depthwise_conv(ext, w, b):
    c = ext.shape[-1]
    out = lax.conv_general_dilated(ext, w[:, None, :].astype(ext.dtype), window_strides=(1,),
                                   padding="VALID", dimension_numbers=("NWC", "WIO", "NWC"),
                                   feature_group_count=c)
    return out + b


def multiscale_pool(ext, pos0):
    L = ext.shape[1] - POOL_HIST
    xf = ext.astype(jnp.float32)
    cs = jnp.concatenate([jnp.zeros_like(xf[:, :1]), jnp.cumsum(xf, axis=1)], axis=1)
    end = cs[:, POOL_HIST + 1:]
    pos = pos0 + jnp.arange(L, dtype=jnp.int32)
    outs = []
    for gi, w in enumerate(POOL_WINDOWS):
        sl = slice(gi * POOL_GROUP, (gi + 1) * POOL_GROUP)
        start = cs[:, POOL_HIST + 1 - w: POOL_HIST + 1 - w + L, sl]
        cnt = jnp.minimum(pos + 1, w).astype(jnp.float32)[None, :, None]
        outs.append((end[..., sl] - start) / cnt)
    mean = jnp.concatenate(outs, axis=-1)
    return mean - xf[:, POOL_HIST:]


def encoder_layer(x, conv_prev, pool_prev, pos0,
                  w_ffn1_gate, w_ffn1_up, w_ffn1_down, ln1_g, ln1_b,
                  w_in, w_gate, b_gate, conv_w, conv_b, conv_ln_g, conv_ln_b, w_conv_proj,
                  w_pool_group, pool_scale, w_pool_proj, w_out, ln2_g, ln2_b,
                  w_ffn2_gate, w_ffn2_up, w_ffn2_down, ln3_g, ln3_b):
    h = layer_norm(ALPHA * x + 0.5 * swiglu(x, w_ffn1_gate, w_ffn1_up, w_ffn1_down), ln1_g, ln1_b)

    u = h @ w_in
    conv_a = u[..., :D_CONV]
    conv_b_ = u[..., D_CONV:2 * D_CONV]
    pool_in = u[..., 2 * D_CONV:]

    glu = conv_a * jax.nn.sigmoid(conv_b_)
    conv_ext = jnp.concatenate([conv_prev.astype(glu.dtype), glu], axis=1)
    dc = causal_depthwise_conv(conv_ext, conv_w, conv_b)
    branch_conv = jax.nn.silu(layer_norm(dc, conv_ln_g, conv_ln_b)) @ w_conv_proj

    pool_ext = jnp.concatenate([pool_prev.astype(pool_in.dtype), pool_in], axis=1)
    pooled = multiscale_pool(pool_ext, pos0).astype(pool_in.dtype)
    Bsz, L = pooled.shape[0], pooled.shape[1]
    pg = pooled.reshape(Bsz, L, N_POOL_GROUPS, POOL_GROUP)
    pg = jnp.einsum("blgc,gcd->blgd", pg, w_pool_group).reshape(Bsz, L, D_POOL)
    branch_pool = (pg * pool_scale) @ w_pool_proj

    gates = jax.nn.sigmoid(h @ w_gate + b_gate)
    g_conv = gates[..., :D_MODEL]
    g_pool = gates[..., D_MODEL:]
    mixed = (g_conv * branch_conv + g_pool * branch_pool) @ w_out
    h2 = layer_norm(ALPHA * h + mixed, ln2_g, ln2_b)

    y = layer_norm(ALPHA * h2 + 0.5 * swiglu(h2, w_ffn2_gate, w_ffn2_up, w_ffn2_down), ln3_g, ln3_b)
    return y, conv_ext[:, -CONV_HIST:], pool_ext[:, -POOL_HIST:]


def setup_inputs(seed: int = 0) -> dict:
    key = jax.random.key(seed)
    ks = iter(jax.random.split(key, 40))

    def nrm(shape, scale):
        return jax.random.normal(next(ks), shape, jnp.float32) * scale

    def gain(shape):
        return 1.0 + nrm(shape, 0.02)

    L = DEPTH
    return {
        "x_prompt": nrm((BATCH, SEQ, D_MODEL), 1.0),
        "x_sample": nrm((DEC_BATCH, DEC_SEQ, D_MODEL), 1.0),
        "state_conv": nrm((L, DEC_BATCH, CONV_HIST, D_CONV), 0.5),
        "state_pool": nrm((L, DEC_BATCH, POOL_HIST, D_POOL), 1.0),
        "w_ffn1_gate": nrm((L, D_MODEL, D_FF), D_MODEL ** -0.5),
        "w_ffn1_up": nrm((L, D_MODEL, D_FF), D_MODEL ** -0.5),
        "w_ffn1_down": nrm((L, D_FF, D_MODEL), BETA * D_FF ** -0.5),
        "ln1_g": gain((L, D_MODEL)),
        "ln1_b": nrm((L, D_MODEL), 0.02),
        "w_in": nrm((L, D_MODEL, D_IN), D_MODEL ** -0.5),
        "w_gate": nrm((L, D_MODEL, 2 * D_MODEL), D_MODEL ** -0.5),
        "b_gate": nrm((L, 2 * D_MODEL), 0.02),
        "conv_w": nrm((L, CONV_WIDTH, D_CONV), CONV_WIDTH ** -0.5),
        "conv_b": nrm((L, D_CONV), 0.02),
        "conv_ln_g": gain((L, D_CONV)),
        "conv_ln_b": nrm((L, D_CONV), 0.02),
        "w_conv_proj": nrm((L, D_CONV, D_MODEL), D_CONV ** -0.5),
        "w_pool_group": nrm((L, N_POOL_GROUPS, POOL_GROUP, POOL_GROUP), POOL_GROUP ** -0.5),
        "pool_scale": 1.0 + nrm((L, D_POOL), 0.1),
        "w_pool_proj": nrm((L, D_POOL, D_MODEL), D_POOL ** -0.5),
        "w_out": nrm((L, D_MODEL, D_MODEL), BETA * D_MODEL ** -0.5),
        "ln2_g": gain((L, D_MODEL)),
        "ln2_b": nrm((L, D_MODEL), 0.02),
        "w_ffn2_gate": nrm((L, D_MODEL, D_FF), D_MODEL ** -0.5),
        "w_ffn2_up": nrm((L, D_MODEL, D_FF), D_MODEL ** -0.5),
        "w_ffn2_down": nrm((L, D_FF, D_MODEL), BETA * D_FF ** -0.5),
        "ln3_g": gain((L, D_MODEL)),
        "ln3_b": nrm((L, D_MODEL), 0.02),
    }


def reference(x_prompt, x_sample, state_conv, state_pool,
              w_ffn1_gate, w_ffn1_up, w_ffn1_down, ln1_g, ln1_b,
              w_in, w_gate, b_gate, conv_w, conv_b, conv_ln_g, conv_ln_b, w_conv_proj,
              w_pool_group, pool_scale, w_pool_proj, w_out, ln2_g, ln2_b,
              w_ffn2_gate, w_ffn2_up, w_ffn2_down, ln3_g, ln3_b):
    hp = x_prompt
    hs = x_sample
    conv_p_list, conv_s_list, pool_p_list, pool_s_list = [], [], [], []
    for l in range(DEPTH):
        weights = (w_ffn1_gate[l], w_ffn1_up[l], w_ffn1_down[l], ln1_g[l], ln1_b[l],
                   w_in[l], w_gate[l], b_gate[l], conv_w[l], conv_b[l], conv_ln_g[l], conv_ln_b[l],
                   w_conv_proj[l], w_pool_group[l], pool_scale[l], w_pool_proj[l], w_out[l],
                   ln2_g[l], ln2_b[l], w_ffn2_gate[l], w_ffn2_up[l], w_ffn2_down[l], ln3_g[l], ln3_b[l])
        zero_conv = jnp.zeros((hp.shape[0], CONV_HIST, D_CONV), hp.dtype)
        zero_pool = jnp.zeros((hp.shape[0], POOL_HIST, D_POOL), hp.dtype)
        hp, cp, pp = encoder_layer(hp, zero_conv, zero_pool, 0, *weights)
        hs, cs_, ps_ = encoder_layer(hs, state_conv[l], state_pool[l], PAST_LEN, *weights)
        conv_p_list.append(cp)
        conv_s_list.append(cs_)
        pool_p_list.append(pp)
        pool_s_list.append(ps_)
    new_conv_prompt = jnp.stack(conv_p_list, axis=0)
    new_conv_sample = jnp.stack(conv_s_list, axis=0)
    new_pool_prompt = jnp.stack(pool_p_list, axis=0)
    new_pool_sample = jnp.stack(pool_s_list, axis=0)
    return (hp, hs, new_conv_prompt, new_conv_sample, new_pool_prompt, new_pool_sample)
```

```python
import contextlib
import numpy as np
import concourse.bass as bass
import concourse.mybir as mybir
from concourse.bass_utils import run_bass_kernel_spmd

F32 = mybir.dt.float32
BF16 = mybir.dt.bfloat16
AF = mybir.ActivationFunctionType
ALU = mybir.AluOpType

D = 1024
DFF = 2816
DC = 512
NCORES = 8
SEQ = 8192
CHUNK = 2048
HALO = 32
DSEQ = 32
NTOK = 96 + CHUNK
ALPHA = 2.0 ** 0.25
EPS = 1e-5
C_FFN = 0.5 / ALPHA
EPS_FFN = EPS / (ALPHA * ALPHA)
NSLOT = 10
SLOT_ELEMS = 2048
POOL_W = (2, 4, 8, 16)


class Sched:
    ENGS = ("sp", "act", "dve", "pool", "pe")

    def __init__(self, nc, stack, dry=False):
        self.nc = nc
        self.stack = stack
        self.dry = dry
        self.sem = {}
        self.count = {}
        self.prog = {e: [] for e in self.ENGS}
        self.waited = {e: {} for e in self.ENGS}
        self.last_w = {}
        self.readers = {}
        self.dsem = {}
        self.dcount = {}
        self.nins = {e: 0 for e in self.ENGS}
        if not dry:
            for e in self.ENGS:
                self.sem[e] = stack.enter_context(nc.semaphore("c_" + e))
                self.count[e] = 0

    def _wait(self, en, ev):
        sem, val, src = ev
        if src == en and en == "pe":
            return
        w = self.waited[en]
        if w.get(id(sem), 0) >= val:
            return
        self.prog[en].append(("wait", sem, val))
        w[id(sem)] = val

    def _deps(self, en, reads, writes):
        for k in reads:
            if k in self.last_w:
                self._wait(en, self.last_w[k])
        for k in writes:
            if k in self.last_w:
                self._wait(en, self.last_w[k])
            for ev in self.readers.get(k, {}).values():
                self._wait(en, ev)

    def _record(self, ev, reads, writes):
        for k in reads:
            d = self.readers.setdefault(k, {})
            old = d.get(id(ev[0]))
            if old is None or old[1] < ev[1]:
                d[id(ev[0])] = ev
        for k in writes:
            self.last_w[k] = ev
            self.readers[k] = {}

    def op(self, en, fn, reads=(), writes=(), inc=True):
        if self.dry:
            return
        self._deps(en, reads, writes)
        self.nins[en] += 1
        if inc:
            self.count[en] += 1
            ev = (self.sem[en], self.count[en], en)
            self.prog[en].append(("ins", fn, self.sem[en], 1))
        else:
            ev = (self.sem[en], self.count[en] + 1, en)
            self.prog[en].append(("ins", fn, None, 0))
        self._record(ev, reads, writes)

    def dma(self, qen, chan, out, in_, reads=(), writes=(), **kw):
        if self.dry:
            return
        if chan not in self.dsem:
            self.dsem[chan] = self.stack.enter_context(self.nc.semaphore("d_" + chan))
            self.dcount[chan] = 0
        self._deps(qen, reads, writes)
        self.dcount[chan] += 1
        fn = lambda e, out=out, in_=in_, kw=kw: e.dma_start(out=out, in_=in_, **kw)
        self.prog[qen].append(("ins", fn, self.dsem[chan], 16))
        ev = (self.dsem[chan], 16 * self.dcount[chan], "dma")
        self._record(ev, reads, writes)

    def fence(self, chan, keys):
        if self.dry:
            return
        ev = (self.dsem[chan], 16 * self.dcount[chan], "dma")
        for k in keys:
            self.last_w[k] = ev

    def finish(self, en):
        if self.dry:
            return
        for chan, sem in self.dsem.items():
            self._wait(en, (sem, 16 * self.dcount[chan], "dma"))

    def replay(self, en, e):
        for item in self.prog[en]:
            if item[0] == "wait":
                e.wait_ge(item[1], item[2])
            else:
                ins = item[1](e)
                if item[2] is not None:
                    ins.then_inc(item[2], item[3])

    def emit(self, block):
        block.sync(lambda e: self.replay("sp", e))
        block.scalar(lambda e: self.replay("act", e))
        block.vector(lambda e: self.replay("dve", e))
        block.gpsimd(lambda e: self.replay("pool", e))
        block.tensor(lambda e: self.replay("pe", e))


class Ring:
    def __init__(self, S, ring_t, plan=None):
        self.S = S
        self.t = ring_t
        self.collect = plan is None
        self.plan = [] if plan is None else plan
        self.next_get = 0
        self.next_issue = 0
        self.released = set()
        if not self.collect:
            self.pump()

    def _view(self, n, shape):
        sl = self.t[:, (n % NSLOT) * SLOT_ELEMS:(n % NSLOT + 1) * SLOT_ELEMS]
        a, b = shape
        return sl[:, :a * b].rearrange("p (a b) -> p a b", a=a)

    def pump(self):
        while self.next_issue < len(self.plan) and (
                self.next_issue < NSLOT or (self.next_issue - NSLOT) in self.released):
            n = self.next_issue
            src, shape = self.plan[n]
            self.S.dma("pool", "ring%d" % (n % NSLOT), self._view(n, shape), src,
                       writes=[("ring", n % NSLOT)])
            self.next_issue += 1

    def get(self, src, shape):
        n = self.next_get
        self.next_get += 1
        if self.collect:
            self.plan.append((src, shape))
            return n, None, ("ring", n % NSLOT)
        assert n < self.next_issue, "weight ring too small for the live set (piece %d)" % n
        return n, self._view(n, shape), ("ring", n % NSLOT)

    def release(self, n):
        if self.collect:
            return
        self.released.add(n)
        self.pump()


class Tile:
    def __init__(self, name, T, slot, x0, segs, y0, ylo):
        self.name = name
        self.T = T
        self.slot = slot
        self.x0 = x0
        self.NS = (T + 127) // 128
        self.rows = [min(128, T - 128 * s) for s in range(self.NS)]
        self.segs = segs
        self.y0 = y0
        self.ylo = ylo


class Seg:
    def __init__(self, col0, ln, gbuf, pbuf, gkey, pkey, tail=None, halo=False, first=False):
        self.col0 = col0
        self.len = ln
        self.gbuf = gbuf
        self.pbuf = pbuf
        self.gkey = gkey
        self.pkey = pkey
        self.tail = tail
        self.halo = halo
        self.first = first


def build_program():
    nc = bass.Bass("TRN2", target_bir_lowering=False)

    def din(name, shape):
        return nc.dram_tensor(name, list(shape), F32, kind="ExternalInput").ap()

    xs_d = din("xs", [NTOK, D])
    mask_d = din("mask", [128, 1])
    invc_d = din("invc", [128, 64])
    sconv_d = din("sconv", [2, 30, DC])
    spool_d = din("spool", [2, 15, DC])
    ffn_d = {}
    for l in (1, 2):
        ffn_d[l] = (din("w%dg" % l, [D, DFF]), din("w%du" % l, [D, DFF]), din("w%dd" % l, [DFF, D]))
    lng_d = {l: din("ln%dg" % l, [1, D]) for l in (1, 2, 3)}
    lnb_d = {l: din("ln%db" % l, [1, D]) for l in (1, 2, 3)}
    win_d = din("win", [D, 3 * DC])
    wgt_d = din("wgt", [D, 2 * D])
    bgt_d = din("bgt", [2 * D])
    cw_d = din("cw", [31, DC])
    cb_d = din("cb", [DC])
    clg_d = din("clg", [DC])
    clb_d = din("clb", [DC])
    wcp_d = din("wcp", [DC, D])
    wpg_d = din("wpg", [4, 128, 128])
    psc_d = din("psc", [DC])
    wpp_d = din("wpp", [DC, D])
    wo_d = din("wo", [D, D])
    y_d = nc.dram_tensor("y", [64 + CHUNK, D], F32, kind="ExternalOutput").ap()
    cs_d = nc.dram_tensor("cs", [3, 32, DC], F32, kind="ExternalOutput").ap()
    pso_d = nc.dram_tensor("pso", [3, 16, DC], F32, kind="ExternalOutput").ap()

    with contextlib.ExitStack() as st:
        def sb(name, shape, dt=F32):
            return st.enter_context(nc.sbuf_tensor("sb_" + name, list(shape), dt))

        R = [sb("R0", [128, 1, D]), sb("R1", [128, 4, D]), sb("R2", [128, 4, D])]
        XT = [sb("XT0", [128, 8, 96], BF16), sb("XT1", [128, 8, 512], BF16), sb("XT2", [128, 8, 512], BF16)]
        ring_t = sb("ring", [128, NSLOT * SLOT_ELEMS], BF16)
        lnA_g, lnA_b = sb("lnA_g", [128, D]), sb("lnA_b", [128, D])
        lng = {1: lnA_g, 2: sb("lng2", [128, D]), 3: lnA_g}
        lnb = {1: lnA_b, 2: sb("lnb2", [128, D]), 3: lnA_b}
        lnk = {1: "A", 2: 2, 3: "A"}
        lngc = {l: sb("lngc%d" % l, [128, 8]) for l in (1, 2)}
        lnbc = {l: sb("lnbc%d" % l, [128, 8]) for l in (1, 2)}
        actdcs = sb("actdcs", [128, 2048], F32)
        dcs = actdcs[:, 0:2048].rearrange("p (c t) -> p c t", c=4)
        actb = [actdcs[:, i * 1024:(i + 1) * 1024].bitcast(BF16).rearrange("p (j t) -> p j t", j=4)
                for i in range(2)]
        sg = [sb("sg%d" % i, [128, 512]) for i in range(2)]
        yn = sg
        gt = [sb("gt%d" % i, [128, 512]) for i in range(4)]
        mixT = sb("mixT", [128, 8, 512], BF16)
        zT = sb("zT", [128, 4, 512], BF16)
        pooled = sb("pooled", [128, 4, 512], BF16)
        pgs = sb("pgs", [128, 4, 512], BF16)
        gluP = [sb("gluP%d" % i, [128, 4, 30 + 512], BF16) for i in range(2)]
        pinP = [sb("pinP%d" % i, [128, 4, 15 + 512]) for i in range(2)]
        gluX = [sb("gluX%d" % i, [128, 4, 30 + 32], BF16) for i in range(3)]
        pinX = [sb("pinX%d" % i, [128, 4, 15 + 32]) for i in range(3)]
        gl32 = [sb("gl32_%d" % i, [128, 4, 32]) for i in range(3)]
        ptmp = [sb("ptmp%d" % i, [128, 528]) for i in range(2)]
        pfx = sb("pfx", [128, 16])
        diag = [sb("diag%d" % i, [128, 31, 128], BF16) for i in range(2)]
        ident = sb("ident", [128, 128])
        identb = sb("identb", [128, 128], BF16)
        cw = sb("cwT", [128, 4, 31])
        cw_raw = ptmp[0]
        sc_raw = [gt[0], gt[1]]
        sp_raw = [gt[2], gt[3]]
        cbv = sb("cbv", [128, 4])
        clgv = sb("clgv", [128, 4])
        clbv = sb("clbv", [128, 4])
        pscv = sb("pscv", [128, 4])
        bgtv = sb("bgtv", [128, 16])
        maskv = sb("maskv", [128, 1])
        invcv = sb("invcv", [128, 64])
        wpg = sb("wpg", [128, 4, 128], BF16)
        stt = sb("stt", [128, 4, 12])
        mv = sb("mv", [128, 4, 2])
        rstd = sb("rstd", [128, 4])
        nmr = sb("nmr", [128, 4])
        stt2 = sb("stt2", [128, 4, 6])
        mv2 = sb("mv2", [128, 4, 2])
        rstd2 = sb("rstd2", [128, 4])
        nmr2 = sb("nmr2", [128, 4])
        ps = st.enter_context(nc.psum_tensor("ps", [128, 8, 512], F32))
        block = st.enter_context(nc.Block())

        segS = [Seg(0, 32, gluX[0], pinX[0], "gX0", "pX0", halo=True),
                Seg(32, 32, gluX[1], pinX[1], "gX1", "pX1", tail=1),
                Seg(64, 32, gluX[2], pinX[2], "gX2", "pX2", tail=2)]
        tS = Tile("S", 96, 0, 0, segS, 0, 32)
        tP = []
        for i in range(4):
            seg = Seg(0, 512, gluP[i % 2], pinP[i % 2], "gP%d" % (i % 2), "pP%d" % (i % 2),
                      tail=(0 if i == 3 else None), first=(i == 0))
            tP.append(Tile("P%d" % i, 512, 1 + i % 2, 96 + 512 * i, [seg], 64 + 512 * i, 0))
        groups = [[tS, tP[0], tP[1]], [tP[2], tP[3]]]
        ffq = [(0, 4), (4, 4), (8, 4), (12, 4), (16, 4), (20, 2)]

        def generate(S, ring):
            state = {"bank": 0, "sgi": 0, "pair": 0, "diag": 0, "reserved": set(), "lnA": 0, "preloaded": set(), "prepped": False, "next_in_slot": {}}

            def bank():
                while True:
                    state["bank"] = (state["bank"] + 1) % 8
                    if state["bank"] not in state["reserved"]:
                        return state["bank"]

            def mm(out, lhsT, rhs, start, stop, reads, writes, inc):
                S.op("pe", lambda e: e.matmul(out, lhsT=lhsT, rhs=rhs, start=start, stop=stop),
                     reads=reads, writes=writes, inc=inc)

            def tr(out, in_, idn, reads, writes, inc):
                S.op("pe", lambda e: e.transpose(out, in_, idn), reads=reads, writes=writes, inc=inc)

            def first_load(t):
                S.dma("sp", "x%d" % t.slot,
                      R[t.slot][:t.T, 0, :] if t.NS == 1 else R[t.slot][:, :, :],
                      xs_d[t.x0:t.x0 + t.T, :] if t.NS == 1 else
                      xs_d[t.x0:t.x0 + t.T, :].rearrange("(s p) d -> p s d", p=128),
                      writes=[("R", t.slot, s_) for s_ in range(t.NS)])
                state["preloaded"].add(t.name)

            for t in groups[0][:2]:
                first_load(t)
            S.op("pool", lambda e: e.memset(ident[:, :], 0.0), writes=["ident"])
            S.op("pool", lambda e: e.affine_select(out=ident[:, :], in_=ident[:, :], compare_op=ALU.not_equal,
                                                  fill=1.0, base=0, pattern=[[-1, 128]], channel_multiplier=1),
                 reads=["ident"], writes=["ident"])
            S.op("dve", lambda e: e.tensor_copy(out=identb[:, :], in_=ident[:, :]), reads=["ident"], writes=["identb"])
            pk = []
            S.dma("sp", "par", lng[2][:, :], lng_d[2].partition_broadcast(128), writes=[("lng", 2)])
            S.dma("sp", "par", lnb[2][:, :], lnb_d[2].partition_broadcast(128), writes=[("lnb", 2)])
            pk += [("lng", 2), ("lnb", 2)]
            for (t_, d_, k_, key) in ((cbv, cb_d, 4, "cbv"), (clgv, clg_d, 4, "clgv"), (clbv, clb_d, 4, "clbv"),
                                      (pscv, psc_d, 4, "pscv"), (bgtv, bgt_d, 16, "bgtv")):
                S.dma("sp", "par", t_[:, :], d_.rearrange("(k p) -> p k", p=128), writes=[key],
                      allow_slow_non_contiguous=True)
                pk.append(key)
            for l in (1, 2):
                S.dma("sp", "par", lngc[l][:, :], lng_d[l].rearrange("o (k p) -> p (o k)", p=128), writes=[("lngc", l)],
                      allow_slow_non_contiguous=True)
                S.dma("sp", "par", lnbc[l][:, :], lnb_d[l].rearrange("o (k p) -> p (o k)", p=128), writes=[("lnbc", l)],
                      allow_slow_non_contiguous=True)
                pk += [("lngc", l), ("lnbc", l)]
            for t in groups[0][2:]:
                first_load(t)
            S.dma("sp", "par", maskv[:, :], mask_d, writes=["maskv"])
            S.dma("sp", "par", invcv[:, :], invc_d, writes=["invcv"])
            S.dma("sp", "par", cw_raw[:31, 0:DC], cw_d, writes=["ptmp"])
            pk += ["maskv", "invcv", "ptmp"]
            for i in range(2):
                S.dma("sp", "par", sc_raw[i][:30, :], sconv_d[i], writes=[("gt", i)])
                S.dma("sp", "par", sp_raw[i][:15, :], spool_d[i], writes=[("gt", 2 + i)])
                pk += [("gt", i), ("gt", 2 + i)]
            S.fence("par", pk)
            S.dma("pool", "parg", wpg[:, :, :], wpg_d.rearrange("g c d -> c g d"), writes=["wpg"])
            S.op("dve", lambda e: e.memset(dcs[:, :, 0:32], 0.0), writes=[("dcs", c) for c in range(4)] + [("act", 0), ("act", 1)])

            def prep_consts():
              if state["prepped"]:
                  return
              state["prepped"] = True
              b = bank()
              for c in range(4):
                  tr(ps[:, b, c * 32:c * 32 + 31], cw_raw[:31, c * 128:(c + 1) * 128], ident[:31, :31],
                     ["ptmp", "ident"], [("ps", b)], c == 3)
              S.op("dve", lambda e, b=b: e.tensor_copy(
                  out=cw[:, :, :], in_=ps[:, b, 0:128].rearrange("p (c k) -> p c k", c=4)[:, :, 0:31]),
                  reads=[("ps", b)], writes=["cw"])
              S.op("dve", lambda e: e.memset(gluX[0][:, :, 0:30], 0.0), writes=[("gX0", "h")])
              S.op("dve", lambda e: e.memset(pinX[0][:, :, 0:15], 0.0), writes=[("pX0", "h")])
              for i in range(2):
                  b = bank()
                  for c in range(4):
                      tr(ps[:, b, c * 32:c * 32 + 30], sc_raw[i][:30, c * 128:(c + 1) * 128], ident[:30, :30],
                         [("gt", i), "ident"], [("ps", b)], c == 3)
                  S.op("dve", lambda e, b=b, i=i: e.tensor_copy(
                      out=gluX[1 + i][:, :, 0:30], in_=ps[:, b, 0:128].rearrange("p (c k) -> p c k", c=4)[:, :, 0:30]),
                      reads=[("ps", b)], writes=[("gX%d" % (1 + i), "h")])
                  b = bank()
                  for c in range(4):
                      tr(ps[:, b, c * 16:c * 16 + 15], sp_raw[i][:15, c * 128:(c + 1) * 128], ident[:15, :15],
                         [("gt", 2 + i), "ident"], [("ps", b)], c == 3)
                  S.op("dve", lambda e, b=b, i=i: e.tensor_copy(
                      out=pinX[1 + i][:, :, 0:15], in_=ps[:, b, 0:64].rearrange("p (c k) -> p c k", c=4)[:, :, 0:15]),
                      reads=[("ps", b)], writes=[("pX%d" % (1 + i), "h")])

            def rkeys(t):
                return [("R", t.slot, s) for s in range(t.NS)]

            def load_x(t):
                if t.name in state["preloaded"]:
                    state["preloaded"].discard(t.name)
                    return
                Rt = R[t.slot]
                if t.NS == 1:
                    S.dma("sp", "x%d" % t.slot, Rt[:t.T, 0, :], xs_d[t.x0:t.x0 + t.T, :], writes=rkeys(t))
                else:
                    S.dma("sp", "x%d" % t.slot, Rt[:, :, :],
                          xs_d[t.x0:t.x0 + t.T, :].rearrange("(s p) d -> p s d", p=128), writes=rkeys(t))

            def store_y(t):
                Rt = R[t.slot]
                if t.NS == 1:
                    S.dma("sp", "y%d" % t.slot, y_d[t.y0:t.y0 + t.T - t.ylo, :], Rt[t.ylo:t.T, 0, :], reads=rkeys(t))
                else:
                    S.dma("sp", "y%d" % t.slot, y_d[t.y0:t.y0 + t.T, :].rearrange("(s p) d -> p s d", p=128),
                          Rt[:, :, :], reads=rkeys(t))

            def load_ln(l):
                if state["lnA"] == l:
                    return
                state["lnA"] = l
                S.dma("sp", "lnAg", lnA_g[:, :], lng_d[l].partition_broadcast(128), writes=[("lng", "A")])
                S.dma("sp", "lnAb", lnA_b[:, :], lnb_d[l].partition_broadcast(128), writes=[("lnb", "A")])

            def transposes(t, aff=None):
                Rt, Xt = R[t.slot], XT[t.slot]
                for k in range(8):
                    b = bank()
                    for s in range(t.NS):
                        r = t.rows[s]
                        tr(ps[:, b, s * 128:s * 128 + r], Rt[:r, s, k * 128:(k + 1) * 128], ident[:r, :r],
                           [("R", t.slot, s), "ident"], [("ps", b)], s == t.NS - 1)
                    if aff is None:
                        S.op("act", lambda e, b=b, k=k: e.activation(out=Xt[:, k, :t.T], in_=ps[:, b, :t.T], func=AF.Copy),
                             reads=[("ps", b)], writes=[("XT", t.slot, k)])
                    else:
                        S.op("act", lambda e, b=b, k=k: e.activation(out=Xt[:, k, :t.T], in_=ps[:, b, :t.T], func=AF.Identity,
                                                                   bias=lnbc[aff][:, k:k + 1], scale=lngc[aff][:, k:k + 1]),
                             reads=[("ps", b), ("lngc", aff), ("lnbc", aff)], writes=[("XT", t.slot, k)])

            def ln_affine(t, l):
                Rt = R[t.slot]
                for s in range(t.NS):
                    r = t.rows[s]
                    key = ("R", t.slot, s)
                    S.op("dve", lambda e, s=s, r=r: e.tensor_tensor(out=Rt[:r, s, :], in0=Rt[:r, s, :], in1=lng[l][:r, :],
                                                                  op=ALU.mult), reads=[key, ("lng", lnk[l])], writes=[key])
                    S.op("dve", lambda e, s=s, r=r: e.tensor_tensor(out=Rt[:r, s, :], in0=Rt[:r, s, :], in1=lnb[l][:r, :],
                                                                  op=ALU.add), reads=[key, ("lnb", lnk[l])], writes=[key])

            def ln_part1(t, eps):
                Rt = R[t.slot]
                for s in range(t.NS):
                    r = t.rows[s]
                    for h in range(2):
                        S.op("dve", lambda e, s=s, r=r, h=h: e.bn_stats(stt[:r, s, h * 6:(h + 1) * 6],
                                                                      Rt[:r, s, h * 512:(h + 1) * 512]),
                             reads=[("R", t.slot, s)], writes=[("stt", s)])
                    S.op("dve", lambda e, s=s, r=r: e.bn_aggr(mv[:r, s, :], stt[:r, s, :]),
                         reads=[("stt", s)], writes=["mv"])
                r0 = t.rows[0] if t.NS == 1 else 128
                ns = t.NS
                S.op("dve", lambda e: e.tensor_scalar(out=rstd[:r0, :ns], in0=mv[:r0, :ns, 1], scalar1=eps,
                                                      scalar2=None, op0=ALU.add), reads=["mv"], writes=["rstd"])
                S.op("dve", lambda e: e.reciprocal(out=rstd[:r0, :ns], in_=rstd[:r0, :ns]), reads=["rstd"], writes=["rstd"])

            def ln_part2(t, l, affine):
                Rt = R[t.slot]
                r0 = t.rows[0] if t.NS == 1 else 128
                ns = t.NS
                S.op("act", lambda e: e.activation(out=rstd[:r0, :ns], in_=rstd[:r0, :ns], func=AF.Sqrt),
                     reads=["rstd"], writes=["rstd"])
                S.op("dve", lambda e: e.scalar_tensor_tensor(out=nmr[:r0, :ns], in0=mv[:r0, :ns, 0], scalar=-1.0,
                                                             in1=rstd[:r0, :ns], op0=ALU.mult, op1=ALU.mult),
                     reads=["mv", "rstd"], writes=["nmr"])
                for s in range(t.NS):
                    r = t.rows[s]
                    key = ("R", t.slot, s)
                    S.op("act", lambda e, s=s, r=r: e.activation(out=Rt[:r, s, :], in_=Rt[:r, s, :], func=AF.Identity,
                                                               bias=nmr[:r, s:s + 1], scale=rstd[:r, s:s + 1]),
                         reads=[key, "rstd", "nmr"], writes=[key])
                if affine:
                    ln_affine(t, l)

            def layer_norm(t, l, eps, affine=True):
                ln_part1(t, eps)
                ln_part2(t, l, affine)

            def ffn_gen(tiles, l, final, first_tr=None):
                wg_d, wu_d, wd_d = ffn_d[l]
                nq = len(ffq)
                pairs = ([(q, t) for q in range(nq - 2) for t in tiles]
                         + [(q, t) for t in tiles for q in (nq - 2, nq - 1)])
                pieces = {}

                def gu(q, t):
                    j0, nj = ffq[q]
                    if q not in pieces:
                        gp, up = [], []
                        for h in range((nj + 1) // 2):
                            c0 = (j0 + 2 * h) * 128
                            w = min(256, (j0 + nj) * 128 - c0)
                            gp.append(ring.get(wg_d[:, c0:c0 + w].rearrange("(k p) f -> p k f", p=128), (8, w)))
                            up.append(ring.get(wu_d[:, c0:c0 + w].rearrange("(k p) f -> p k f", p=128), (8, w)))
                        pieces[q] = {"g": gp, "u": up}
                    P = pieces[q]
                    if q == 0 and first_tr == "x":
                        transposes(t)
                    if q == 0 and first_tr == "h2":
                        transposes(t, aff=2)
                        ln_affine(t, 2)
                    Xt = XT[t.slot]
                    ab = state["pair"] % 2
                    state["pair"] += 1
                    for jj in range(nj):
                        bg, bu = bank(), bank()
                        for (bnk, pl) in ((bg, P["g"]), (bu, P["u"])):
                            n, v, key = pl[jj // 2]
                            for k in range(8):
                                mm(ps[:, bnk, :t.T], None if v is None else v[:, k, (jj % 2) * 128:(jj % 2 + 1) * 128],
                                   Xt[:, k, :t.T], k == 0, k == 7, [key, ("XT", t.slot, k)], [("ps", bnk)], k == 7)
                        si = state["sgi"] % 2
                        state["sgi"] += 1
                        S.op("act", lambda e, bg=bg, si=si: e.activation(out=sg[si][:, :t.T], in_=ps[:, bg, :t.T], func=AF.Silu),
                             reads=[("ps", bg)], writes=[("sg", si)])
                        S.op("dve", lambda e, bu=bu, si=si, jj=jj, ab=ab: e.tensor_tensor(
                            out=actb[ab][:, jj, :t.T], in0=ps[:, bu, :t.T], in1=sg[si][:, :t.T], op=ALU.mult),
                            reads=[("ps", bu), ("sg", si)], writes=[("act", ab)])
                    if t is tiles[-1]:
                        for pl in (P["g"], P["u"]):
                            for (n, v, key) in pl:
                                ring.release(n)
                    return ab

                def down(q, t, ab):
                    j0, nj = ffq[q]
                    P = pieces[q]
                    if "d" not in P:
                        P["d"] = [ring.get(wd_d[j0 * 128:(j0 + nj) * 128, f * 512:(f + 1) * 512]
                                           .rearrange("(j p) n -> p j n", p=128), (nj, 512)) for f in range(2)]
                    Rt = R[t.slot]
                    for f in range(2):
                        n, v, key = P["d"][f]
                        for s in range(t.NS):
                            r = t.rows[s]
                            b = bank()
                            for jj in range(nj):
                                mm(ps[:r, b, :512], actb[ab][:, jj, s * 128:s * 128 + r],
                                   None if v is None else v[:, jj, :], jj == 0, jj == nj - 1,
                                   [key, ("act", ab)], [("ps", b)], jj == nj - 1)
                            rk = ("R", t.slot, s)
                            S.op("dve", lambda e, b=b, s=s, r=r, f=f: e.scalar_tensor_tensor(
                                out=Rt[:r, s, f * 512:(f + 1) * 512], in0=ps[:r, b, :512], scalar=C_FFN,
                                in1=Rt[:r, s, f * 512:(f + 1) * 512], op0=ALU.mult, op1=ALU.add),
                                reads=[("ps", b), rk], writes=[rk])
                    if t is tiles[-1]:
                        for (n, v, key) in P["d"]:
                            ring.release(n)

                def ln_finish(t):
                    if not final:
                        ln_part2(t, 1, False)
                    else:
                        Rt = R[t.slot]
                        r0 = t.rows[0] if t.NS == 1 else 128
                        ns = t.NS
                        S.op("act", lambda e: e.activation(out=rstd[:r0, :ns], in_=rstd[:r0, :ns], func=AF.Sqrt),
                             reads=["rstd"], writes=["rstd"])
                        S.op("dve", lambda e: e.scalar_tensor_tensor(out=nmr[:r0, :ns], in0=mv[:r0, :ns, 0], scalar=-1.0,
                                                                     in1=rstd[:r0, :ns], op0=ALU.mult, op1=ALU.mult),
                             reads=["mv", "rstd"], writes=["nmr"])
                        for s in range(t.NS):
                            r = t.rows[s]
                            key = ("R", t.slot, s)
                            S.op("act", lambda e, s=s, r=r: e.activation(out=Rt[:r, s, :], in_=Rt[:r, s, :], func=AF.Identity,
                                                                       bias=nmr[:r, s:s + 1], scale=rstd[:r, s:s + 1]),
                                 reads=[key, "rstd", "nmr"], writes=[key])
                            S.op("dve", lambda e, s=s, r=r: e.tensor_tensor(out=Rt[:r, s, :], in0=Rt[:r, s, :], in1=lng[3][:r, :],
                                                                          op=ALU.mult), reads=[key, ("lng", "A")], writes=[key])
                            S.op("dve", lambda e, s=s, r=r: e.tensor_tensor(out=Rt[:r, s, :], in0=Rt[:r, s, :], in1=lnb[3][:r, :],
                                                                          op=ALU.add), reads=[key, ("lnb", "A")], writes=[key])
                            if t.NS == 1:
                                S.dma("sp", "y%d_%d" % (t.slot, s), y_d[t.y0:t.y0 + t.T - t.ylo, :], Rt[t.ylo:t.T, 0, :],
                                      reads=[key])
                            else:
                                S.dma("sp", "y%d_%d" % (t.slot, s), y_d[t.y0 + s * 128:t.y0 + s * 128 + r, :], Rt[:r, s, :],
                                      reads=[key])
                    if final:
                        nt = state["next_in_slot"].get(t.slot)
                        if nt is not None:
                            state["next_in_slot"][t.slot] = None
                            load_x(nt)
                            state["preloaded"].add(nt.name)

                pend = None
                ln_pend = None
                abs_ = {}
                abs_[0] = gu(*pairs[0])
                yield ("prologue", pairs[0])
                for i in range(len(pairs)):
                    if i + 1 < len(pairs):
                        abs_[i + 1] = gu(*pairs[i + 1])
                    down(pairs[i][0], pairs[i][1], abs_[i])
                    if pend is not None:
                        transposes(pend, aff=1)
                        yield ("tr", pend)
                        pend = None
                    if ln_pend is not None:
                        ln_finish(ln_pend)
                        if not final:
                            pend = ln_pend
                        ln_pend = None
                    if pairs[i][0] == len(ffq) - 1:
                        ln_part1(pairs[i][1], EPS_FFN)
                        ln_pend = pairs[i][1]
                    yield ("step", pairs[i])
                if ln_pend is not None:
                    ln_finish(ln_pend)
                    if not final:
                        if pend is not None:
                            transposes(pend, aff=1)
                        pend = ln_pend
                if pend is not None:
                    yield ("defer_tr", pend)

            def build_diag(c):
                i = state["diag"] % 2
                state["diag"] += 1
                i0 = identb[:, :].unsqueeze(1).broadcast_to([128, 31, 128])
                i1 = cw[:, c, :].unsqueeze(2).broadcast_to([128, 31, 128])
                S.op("dve", lambda e, i=i: e.tensor_tensor(out=diag[i][:, :, :], in0=i0, in1=i1, op=ALU.mult),
                     reads=["identb", "cw"], writes=[("diag", i)])
                return i

            def stage_a(cx):
                t = cx["t"]
                T = t.T
                Xt = XT[t.slot]
                xk = [("XT", t.slot, k) for k in range(8)]
                wi = {}
                prep_consts()
                dnext = build_diag(0)

                def win_piece(i):
                    if i not in wi:
                        wi[i] = ring.get(win_d[:, i * 256:(i + 1) * 256].rearrange("(k p) f -> p k f", p=128), (8, 256))
                    return wi[i]

                def proj_in(m):
                    n, v, key = win_piece(m // 2)
                    b = bank()
                    for k in range(8):
                        mm(ps[:, b, :T], None if v is None else v[:, k, (m % 2) * 128:(m % 2 + 1) * 128],
                           Xt[:, k, :T], k == 0, k == 7, [key, xk[k]], [("ps", b)], k == 7)
                    return b

                for c in range(4):
                    ba = proj_in(c)
                    bb = proj_in(4 + c)
                    si = state["sgi"] % 2
                    state["sgi"] += 1
                    S.op("act", lambda e, bb=bb, si=si: e.activation(out=sg[si][:, :T], in_=ps[:, bb, :T], func=AF.Sigmoid),
                         reads=[("ps", bb)], writes=[("sg", si)])
                    for sgm in t.segs:
                        c0, ln = sgm.col0, sgm.len
                        S.op("dve", lambda e, ba=ba, si=si, c=c, sgm=sgm, c0=c0, ln=ln: e.tensor_tensor(
                            out=sgm.gbuf[:, c, 30:30 + ln], in0=ps[:, ba, c0:c0 + ln], in1=sg[si][:, c0:c0 + ln], op=ALU.mult),
                            reads=[("ps", ba), ("sg", si)], writes=[(sgm.gkey, c)])
                        if sgm.tail is not None:
                            S.op("dve", lambda e, ba=ba, si=si, c=c, sgm=sgm, c0=c0, ln=ln: e.tensor_tensor(
                                out=gl32[sgm.tail][:, c, :], in0=ps[:, ba, c0 + ln - 32:c0 + ln],
                                in1=sg[si][:, c0 + ln - 32:c0 + ln], op=ALU.mult),
                                reads=[("ps", ba), ("sg", si)], writes=[("gl32", sgm.tail)])
                        if sgm.halo:
                            S.op("dve", lambda e, c=c, sgm=sgm: e.tensor_scalar(
                                out=gluP[0][:, c, 0:30], in0=sgm.gbuf[:, c, 32:62], scalar1=maskv[:, 0:1], scalar2=None,
                                op0=ALU.mult), reads=[(sgm.gkey, c), "maskv"], writes=[("gP0", "h")])
                    if c % 2 == 1:
                        ring.release(win_piece(c // 2)[0])
                        ring.release(win_piece(2 + c // 2)[0])
                yield "A1"
                for c in range(4):
                    b = proj_in(8 + c)
                    for sgm in t.segs:
                        c0, ln = sgm.col0, sgm.len
                        S.op("act", lambda e, b=b, c=c, sgm=sgm, c0=c0, ln=ln: e.activation(
                            out=sgm.pbuf[:, c, 15:15 + ln], in_=ps[:, b, c0:c0 + ln], func=AF.Copy),
                            reads=[("ps", b)], writes=[(sgm.pkey, c)])
                        if sgm.halo:
                            S.op("dve", lambda e, c=c, sgm=sgm: e.tensor_scalar(
                                out=pinP[0][:, c, 0:15], in0=sgm.pbuf[:, c, 32:47], scalar1=maskv[:, 0:1], scalar2=None,
                                op0=ALU.mult), reads=[(sgm.pkey, c), "maskv"], writes=[("pP0", "h")])
                    if c % 2 == 1:
                        ring.release(win_piece(4 + c // 2)[0])
                if any(sgm.halo for sgm in t.segs):
                    S.op("dve", lambda e: e.memset(dcs[:, :, 0:32], 0.0),
                         writes=[("dcs", c) for c in range(4)] + [("act", 0), ("act", 1)])
                for c in range(4):
                    di = dnext
                    b = bank()
                    csegs = [sgm for sgm in t.segs if not sgm.halo]
                    lo_c = csegs[0].col0
                    for si_, sgm in enumerate(csegs):
                        c0, ln = sgm.col0, sgm.len
                        for k in range(31):
                            mm(ps[:, b, c0:c0 + ln], diag[di][:, k, :], sgm.gbuf[:, c, k:k + ln], k == 0, k == 30,
                               [("diag", di), (sgm.gkey, c), (sgm.gkey, "h")], [("ps", b)],
                               k == 30 and si_ == len(csegs) - 1)
                    if c < 3:
                        dnext = build_diag(c + 1)
                    S.op("act", lambda e, b=b, c=c, lo_c=lo_c: e.activation(out=dcs[:, c, lo_c:T], in_=ps[:, b, lo_c:T],
                                                                          func=AF.Identity, bias=cbv[:, c:c + 1]),
                         reads=[("ps", b), "cbv"], writes=[("dcs", c), ("act", c // 2)])
                ln_affine(t, 1)

            def stage_b1(cx):
                t = cx["t"]
                fb = []
                for s in range(t.NS):
                    r = t.rows[s]
                    b = bank()
                    fb.append(b)
                    for c in range(4):
                        tr(ps[:r, b, c * 128:(c + 1) * 128], dcs[:, c, s * 128:s * 128 + r], ident[:, :],
                           [("dcs", c), ("act", c // 2), "ident"], [("ps", b)], c == 3)
                    S.op("dve", lambda e, s=s, r=r, b=b: e.bn_stats(stt2[:r, s, 0:6], ps[:r, b, :512]),
                         reads=[("ps", b)], writes=[("stt2", s)])
                    S.op("dve", lambda e, s=s, r=r: e.bn_aggr(mv2[:r, s, :], stt2[:r, s, 0:6]),
                         reads=[("stt2", s)], writes=["mv2"])
                cx["fb"] = fb
                state["reserved"] = set(fb)
                r0 = t.rows[0] if t.NS == 1 else 128
                ns = t.NS
                S.op("dve", lambda e: e.tensor_scalar(out=rstd2[:r0, :ns], in0=mv2[:r0, :ns, 1], scalar1=EPS, scalar2=None,
                                                      op0=ALU.add), reads=["mv2"], writes=["rstd2"])
                S.op("dve", lambda e: e.reciprocal(out=rstd2[:r0, :ns], in_=rstd2[:r0, :ns]), reads=["rstd2"], writes=["rstd2"])
                yield "B1a"
                if cx["nxt"] is not None:
                    sg0 = t.segs[0]
                    nb = cx["nxt"].segs[0]
                    S.op("dve", lambda e: e.tensor_copy(out=nb.gbuf[:, :, 0:30], in_=sg0.gbuf[:, :, 512:542]),
                         reads=[(sg0.gkey, c) for c in range(4)], writes=[(nb.gkey, "h")])
                    S.op("dve", lambda e: e.tensor_copy(out=nb.pbuf[:, :, 0:15], in_=sg0.pbuf[:, :, 512:527]),
                         reads=[(sg0.pkey, c) for c in range(4)], writes=[(nb.pkey, "h")])
                for sgm in t.segs:
                    c0, ln = sgm.col0, sgm.len
                    L = 15 + ln
                    for c in range(4):
                        E = sgm.pbuf[:, c, :]
                        src = E
                        off = 1
                        lo = 1
                        for step in range(c + 1):
                            dst = ptmp[step % 2]
                            S.op("dve", lambda e, dst=dst, src=src, lo=lo, off=off, L=L: e.tensor_tensor(
                                out=dst[:, lo:L], in0=src[:, lo:L], in1=src[:, lo - off:L - off], op=ALU.add),
                                reads=[(sgm.pkey, c), (sgm.pkey, "h"), "ptmp"], writes=["ptmp"])
                            src = dst
                            off *= 2
                            lo = 2 * off - 1
                        w = POOL_W[c]
                        S.op("dve", lambda e, src=src, w=w, L=L: e.tensor_scalar(
                            out=src[:, 15:L], in0=src[:, 15:L], scalar1=1.0 / w, scalar2=None, op0=ALU.mult),
                            reads=["ptmp"], writes=["ptmp"])
                        S.op("dve", lambda e, src=src, E=E, c=c, c0=c0, ln=ln, L=L: e.tensor_tensor(
                            out=pooled[:, c, c0:c0 + ln], in0=src[:, 15:L], in1=E[:, 15:L], op=ALU.subtract),
                            reads=["ptmp", (sgm.pkey, c)], writes=[("pooled", c)])
                        if sgm.first:
                            S.op("dve", lambda e, src=src, c=c: e.tensor_tensor(
                                out=pfx[:, :], in0=src[:, 15:31], in1=invcv[:, c * 16:(c + 1) * 16], op=ALU.mult),
                                reads=["ptmp", "invcv"], writes=["pfx"])
                            S.op("dve", lambda e, E=E, c=c, c0=c0: e.tensor_tensor(
                                out=pooled[:, c, c0:c0 + 16], in0=pfx[:, :], in1=E[:, 15:31], op=ALU.subtract),
                                reads=["pfx", (sgm.pkey, c)], writes=[("pooled", c)])
                        yield "pool"
                for sgm in t.segs:
                    if sgm.tail is None:
                        continue
                    i = sgm.tail
                    L = 15 + sgm.len
                    j0 = 2 if i == 2 else 0
                    so_c, so_p = gt[j0][:32, :], gt[j0 + 1][:16, :]
                    kc, kp = ("gt", j0), ("gt", j0 + 1)
                    b = bank()
                    for c in range(4):
                        tr(ps[:32, b, c * 128:(c + 1) * 128], gl32[i][:, c, :], ident[:, :],
                           [("gl32", i), "ident"], [("ps", b)], c == 3)
                    S.op("act", lambda e, b=b, so_c=so_c: e.activation(out=so_c, in_=ps[:32, b, :512], func=AF.Copy),
                         reads=[("ps", b)], writes=[kc])
                    S.dma("sp", "so_c%d" % j0, cs_d[i], so_c, reads=[kc])
                    b = bank()
                    for c in range(4):
                        tr(ps[:16, b, c * 128:(c + 1) * 128], sgm.pbuf[:, c, L - 16:L], ident[:, :],
                           [(sgm.pkey, c), "ident"], [("ps", b)], c == 3)
                    S.op("act", lambda e, b=b, so_p=so_p: e.activation(out=so_p, in_=ps[:16, b, :512], func=AF.Copy),
                         reads=[("ps", b)], writes=[kp])
                    S.dma("sp", "so_p%d" % j0, pso_d[i], so_p, reads=[kp])

            def b2_pre(cx):
                if cx.get("b2pre"):
                    return
                cx["b2pre"] = True
                t = cx["t"]
                r0 = t.rows[0] if t.NS == 1 else 128
                ns = t.NS
                S.op("act", lambda e: e.activation(out=rstd2[:r0, :ns], in_=rstd2[:r0, :ns], func=AF.Sqrt),
                     reads=["rstd2"], writes=["rstd2"])
                S.op("dve", lambda e: e.scalar_tensor_tensor(out=nmr2[:r0, :ns], in0=mv2[:r0, :ns, 0], scalar=-1.0,
                                                             in1=rstd2[:r0, :ns], op0=ALU.mult, op1=ALU.mult),
                     reads=["mv2", "rstd2"], writes=["nmr2"])

            def stage_b2(cx):
                t = cx["t"]
                T = t.T
                fb = cx["fb"]
                b2_pre(cx)
                zb = [bank() for c in range(4)]
                for s in range(t.NS):
                    r = t.rows[s]
                    yi = s % 2
                    S.op("act", lambda e, s=s, r=r, yi=yi: e.activation(out=yn[yi][:r, :], in_=ps[:r, fb[s], :512],
                                                                      func=AF.Identity, bias=nmr2[:r, s:s + 1],
                                                                      scale=rstd2[:r, s:s + 1]),
                         reads=[("ps", fb[s]), "rstd2", "nmr2"], writes=[("sg", yi)])
                    for c in range(4):
                        tr(ps[:, zb[c], s * 128:s * 128 + r], yn[yi][:r, c * 128:(c + 1) * 128], ident[:r, :r],
                           [("sg", yi), "ident"], [("ps", zb[c])], True)
                state["reserved"] = set()
                for c in range(4):
                    S.op("act", lambda e, c=c: e.activation(out=zT[:, c, :T], in_=ps[:, zb[c], :T], func=AF.Silu,
                                                          bias=clbv[:, c:c + 1], scale=clgv[:, c:c + 1]),
                         reads=[("ps", zb[c]), "clgv", "clbv"], writes=[("zT", c)])
                for c in range(4):
                    b = bank()
                    mm(ps[:, b, :T], wpg[:, c, :], pooled[:, c, :T], True, True, ["wpg", ("pooled", c)], [("ps", b)], True)
                    S.op("act", lambda e, b=b, c=c: e.activation(out=pgs[:, c, :T], in_=ps[:, b, :T], func=AF.Identity,
                                                               scale=pscv[:, c:c + 1]),
                         reads=[("ps", b), "pscv"], writes=[("pgs", c)])

            def stage_c(cx):
                t = cx["t"]
                T = t.T
                Xt, Rt = XT[t.slot], R[t.slot]
                xk = [("XT", t.slot, k) for k in range(8)]
                wgp = {}
                prj = {}

                def gate_piece(i):
                    if i not in wgp:
                        wgp[i] = ring.get(wgt_d[:, i * 256:(i + 1) * 256].rearrange("(k p) f -> p k f", p=128), (8, 256))
                    return wgp[i]

                def proj_piece(nm, wd, f):
                    if (nm, f) not in prj:
                        prj[(nm, f)] = ring.get(wd[:, f * 512:(f + 1) * 512].rearrange("(c p) n -> p c n", p=128), (4, 512))
                    return prj[(nm, f)]

                for m in range(8):
                    g0, g1 = gt[(m % 2) * 2], gt[(m % 2) * 2 + 1]
                    k0, k1 = ("gt", (m % 2) * 2), ("gt", (m % 2) * 2 + 1)
                    for (gi, gtile, gkey, nm, wd, rhs_t, rkey) in ((0, g0, k0, "cp", wcp_d, zT, "zT"),
                                                                 (1, g1, k1, "pp", wpp_d, pgs, "pgs")):
                        mg = m + 8 * gi
                        n, v, key = gate_piece(mg // 2)
                        b = bank()
                        for k in range(8):
                            mm(ps[:, b, :T], None if v is None else v[:, k, (mg % 2) * 128:(mg % 2 + 1) * 128],
                               Xt[:, k, :T], k == 0, k == 7, [key, xk[k]], [("ps", b)], k == 7)
                        S.op("act", lambda e, b=b, gtile=gtile, mg=mg: e.activation(
                            out=gtile[:, :T], in_=ps[:, b, :T], func=AF.Sigmoid, bias=bgtv[:, mg:mg + 1]),
                            reads=[("ps", b), "bgtv"], writes=[gkey])
                        n2, v2, key2 = proj_piece(nm, wd, m // 4)
                        b2 = bank()
                        for c in range(4):
                            mm(ps[:, b2, :T], None if v2 is None else v2[:, c, (m % 4) * 128:(m % 4 + 1) * 128],
                               rhs_t[:, c, :T], c == 0, c == 3, [key2, (rkey, c)], [("ps", b2)], c == 3)
                        S.op("dve", lambda e, b2=b2, gtile=gtile: e.tensor_tensor(
                            out=gtile[:, :T], in0=gtile[:, :T], in1=ps[:, b2, :T], op=ALU.mult),
                            reads=[("ps", b2), gkey], writes=[gkey])
                    S.op("dve", lambda e, m=m, g0=g0, g1=g1: e.tensor_tensor(
                        out=mixT[:, m, :T], in0=g0[:, :T], in1=g1[:, :T], op=ALU.add),
                        reads=[k0, k1], writes=[("mixT", m)])
                    if m % 2 == 1:
                        ring.release(gate_piece(m // 2)[0])
                        ring.release(gate_piece(4 + m // 2)[0])
                    if m % 4 == 3:
                        ring.release(proj_piece("cp", wcp_d, m // 4)[0])
                        ring.release(proj_piece("pp", wpp_d, m // 4)[0])
                    yield "m"
                for f in range(2):
                    wop = [ring.get(wo_d[kh * 512:(kh + 1) * 512, f * 512:(f + 1) * 512]
                                    .rearrange("(k p) n -> p k n", p=128), (4, 512)) for kh in range(2)]
                    for s in range(t.NS):
                        r = t.rows[s]
                        b = bank()
                        for k in range(8):
                            n, v, key = wop[k // 4]
                            mm(ps[:r, b, :512], mixT[:, k, s * 128:s * 128 + r], None if v is None else v[:, k % 4, :],
                               k == 0, k == 7, [key, ("mixT", k)], [("ps", b)], k == 7)
                        rk = ("R", t.slot, s)
                        S.op("dve", lambda e, b=b, s=s, r=r, f=f: e.scalar_tensor_tensor(
                            out=Rt[:r, s, f * 512:(f + 1) * 512], in0=Rt[:r, s, f * 512:(f + 1) * 512], scalar=ALPHA,
                            in1=ps[:r, b, :512], op0=ALU.mult, op1=ALU.add), reads=[("ps", b), rk], writes=[rk])
                    for (n, v, key) in wop:
                        ring.release(n)
                ln_part1(t, EPS)

            def run_group(tiles, next_tiles):
                cxs = []
                for t in tiles:
                    nxt = None
                    if t.name.startswith("P") and t.name != "P3":
                        nxt = tP[int(t.name[1]) + 1]
                    cxs.append({"t": t, "nxt": nxt})
                for t in tiles:
                    load_x(t)
                load_ln(1)
                def fin(g):
                    for _ in g:
                        pass

                deferred = None
                for tag in ffn_gen(tiles, 1, False, first_tr="x"):
                    if tag[0] == "defer_tr":
                        deferred = tag[1]
                n = len(cxs)
                if cxs[0]["t"] is deferred:
                    transposes(deferred, aff=1)
                fin(stage_a(cxs[0]))
                ln2_pend = None
                for i in range(n):
                    if i + 1 < n and cxs[i + 1]["t"] is deferred:
                        transposes(deferred, aff=1)
                    gb = stage_b1(cxs[i])
                    ga = stage_a(cxs[i + 1]) if i + 1 < n else iter(())
                    next(gb, None)
                    next(ga, None)
                    if i + 1 < n or i == 0:
                        fin(gb)
                    fin(ga)
                    if i > 0:
                        if ln2_pend is not None:
                            ln_part2(ln2_pend, 2, False)
                        gc_ = stage_c(cxs[i - 1])
                        next(gc_, None)
                        b2_pre(cxs[i])
                        while True:
                            r1 = next(gb, "END")
                            r2 = next(gc_, "END")
                            if r2 != "END":
                                r2 = next(gc_, "END")
                            if r1 == "END" and r2 == "END":
                                break
                        ln2_pend = cxs[i - 1]["t"]
                    stage_b2(cxs[i])
                if ln2_pend is not None:
                    ln_part2(ln2_pend, 2, False)
                fin(stage_c(cxs[n - 1]))
                load_ln(3)
                for nt in next_tiles:
                    state["next_in_slot"][nt.slot] = nt
                order2 = [t for t in tiles[:-1] if t.T == 512] + [t for t in tiles[:-1] if t.T != 512] + [tiles[-1]]
                g2 = ffn_gen(order2, 2, True, first_tr="h2")
                next(g2)
                if len(order2) >= 3:
                    next(g2)
                ln_part2(tiles[-1], 2, False)
                fin(g2)

            for gi, tiles in enumerate(groups):
                run_group(tiles, groups[gi + 1] if gi + 1 < len(groups) else [])
            S.finish("sp")

        dryS = Sched(nc, st, dry=True)
        plan_ring = Ring(dryS, ring_t, plan=None)
        generate(dryS, plan_ring)
        S = Sched(nc, st)
        ring = Ring(S, ring_t, plan=plan_ring.plan)
        generate(S, ring)
        assert ring.next_get == len(ring.plan) == ring.next_issue, (ring.next_get, len(ring.plan), ring.next_issue)
        S.emit(block)
        build_program.stats = (dict(S.nins), {e: len(S.prog[e]) for e in S.ENGS}, len(ring.plan))
    return nc


_CACHE = {}


def kernel(x_prompt, x_sample, state_conv, state_pool,
           w_ffn1_gate, w_ffn1_up, w_ffn1_down, ln1_g, ln1_b,
           w_in, w_gate, b_gate, conv_w, conv_b, conv_ln_g, conv_ln_b, w_conv_proj,
           w_pool_group, pool_scale, w_pool_proj, w_out, ln2_g, ln2_b,
           w_ffn2_gate, w_ffn2_up, w_ffn2_down, ln3_g, ln3_b):
    f = lambda a: np.ascontiguousarray(np.asarray(a, dtype=np.float32))
    x_prompt, x_sample = f(x_prompt), f(x_sample)
    state_conv, state_pool = f(state_conv), f(state_pool)
    shared = {
        "w1g": f(w_ffn1_gate)[0], "w1u": f(w_ffn1_up)[0], "w1d": f(w_ffn1_down)[0],
        "w2g": f(w_ffn2_gate)[0], "w2u": f(w_ffn2_up)[0], "w2d": f(w_ffn2_down)[0],
        "ln1g": f(ln1_g), "ln1b": f(ln1_b), "ln2g": f(ln2_g), "ln2b": f(ln2_b), "ln3g": f(ln3_g), "ln3b": f(ln3_b),
        "win": f(w_in)[0], "wgt": f(w_gate)[0], "bgt": f(b_gate)[0], "cw": f(conv_w)[0], "cb": f(conv_b)[0],
        "clg": f(conv_ln_g)[0], "clb": f(conv_ln_b)[0], "wcp": f(w_conv_proj)[0], "wpg": f(w_pool_group)[0],
        "psc": f(pool_scale)[0], "wpp": f(w_pool_proj)[0], "wo": f(w_out)[0],
    }
    in_maps = []
    for c in range(NCORES):
        b, ch = c // 4, c % 4
        t0 = ch * CHUNK
        xs = np.zeros((NTOK, D), np.float32)
        if ch > 0:
            xs[0:HALO] = x_prompt[b, t0 - HALO:t0]
        xs[32:64] = x_sample[2 * c]
        xs[64:96] = x_sample[2 * c + 1]
        xs[96:] = x_prompt[b, t0:t0 + CHUNK]
        mask = np.full((128, 1), 1.0 if ch > 0 else 0.0, np.float32)
        invc = np.zeros((128, 64), np.float32)
        for g, w in enumerate(POOL_W):
            for p in range(16):
                cnt = min(p + 1, w) if ch == 0 else w
                invc[:, g * 16 + p] = float(w) / cnt
        m = dict(shared)
        m.update({"xs": xs, "mask": mask, "invc": invc,
                  "sconv": np.ascontiguousarray(state_conv[0, 2 * c:2 * c + 2]),
                  "spool": np.ascontiguousarray(state_pool[0, 2 * c:2 * c + 2])})
        in_maps.append(m)
    if "nc" not in _CACHE:
        _CACHE["nc"] = build_program()
    res = run_bass_kernel_spmd(_CACHE["nc"], in_maps, core_ids=list(range(NCORES)))
    y_prompt = np.zeros((2, SEQ, D), np.float32)
    y_sample = np.zeros((16, DSEQ, D), np.float32)
    ncp = np.zeros((1, 2, 30, DC), np.float32)
    ncs = np.zeros((1, 16, 30, DC), np.float32)
    npp = np.zeros((1, 2, 15, DC), np.float32)
    nps = np.zeros((1, 16, 15, DC), np.float32)
    for c in range(NCORES):
        r = res.results[c]
        b, ch = c // 4, c % 4
        y = np.asarray(r["y"])
        cs = np.asarray(r["cs"])
        pso = np.asarray(r["pso"])
        y_prompt[b, ch * CHUNK:(ch + 1) * CHUNK] = y[64:]
        y_sample[2 * c] = y[0:32]
        y_sample[2 * c + 1] = y[32:64]
        ncs[0, 2 * c] = cs[1, 2:32]
        ncs[0, 2 * c + 1] = cs[2, 2:32]
        nps[0, 2 * c] = pso[1, 1:16]
        nps[0, 2 * c + 1] = pso[2, 1:16]
        if ch == 3:
            ncp[0, b] = cs[0, 2:32]
            npp[0, b] = pso[0, 1:16]
    return (y_prompt, y_sample, ncp, ncs, npp, nps)
```

```python
import contextlib
import numpy as np
import concourse.bass as bass
import concourse.mybir as mybir
from concourse.bass_utils import run_bass_kernel_spmd

F32 = mybir.dt.float32
BF16 = mybir.dt.bfloat16
AF = mybir.ActivationFunctionType
ALU = mybir.AluOpType

D = 1024
DFF = 2816
DC = 512
NCORES = 8
SEQ = 8192
CHUNK = 2048
HALO = 32
DSEQ = 32
NTOK = 96 + CHUNK
ALPHA = 2.0 ** 0.25
EPS = 1e-5
C_FFN = 0.5 / ALPHA
EPS_FFN = EPS / (ALPHA * ALPHA)
NSLOT = 10
SLOT_ELEMS = 2048
POOL_W = (2, 4, 8, 16)


class Sched:
    ENGS = ("sp", "act", "dve", "pool", "pe")

    def __init__(self, nc, stack, dry=False):
        self.nc = nc
        self.stack = stack
        self.dry = dry
        self.sem = {}
        self.count = {}
        self.prog = {e: [] for e in self.ENGS}
        self.waited = {e: {} for e in self.ENGS}
        self.last_w = {}
        self.readers = {}
        self.dsem = {}
        self.dcount = {}
        self.nins = {e: 0 for e in self.ENGS}
        if not dry:
            for e in self.ENGS:
                self.sem[e] = stack.enter_context(nc.semaphore("c_" + e))
                self.count[e] = 0

    def _wait(self, en, ev):
        sem, val, src = ev
        if src == en and en == "pe":
            return
        w = self.waited[en]
        if w.get(id(sem), 0) >= val:
            return
        self.prog[en].append(("wait", sem, val))
        w[id(sem)] = val

    def _deps(self, en, reads, writes):
        for k in reads:
            if k in self.last_w:
                self._wait(en, self.last_w[k])
        for k in writes:
            if k in self.last_w:
                self._wait(en, self.last_w[k])
            for ev in self.readers.get(k, {}).values():
                self._wait(en, ev)

    def _record(self, ev, reads, writes):
        for k in reads:
            d = self.readers.setdefault(k, {})
            old = d.get(id(ev[0]))
            if old is None or old[1] < ev[1]:
                d[id(ev[0])] = ev
        for k in writes:
            self.last_w[k] = ev
            self.readers[k] = {}

    def op(self, en, fn, reads=(), writes=(), inc=True):
        if self.dry:
            return
        self._deps(en, reads, writes)
        self.nins[en] += 1
        if inc:
            self.count[en] += 1
            ev = (self.sem[en], self.count[en], en)
            self.prog[en].append(("ins", fn, self.sem[en], 1))
        else:
            ev = (self.sem[en], self.count[en] + 1, en)
            self.prog[en].append(("ins", fn, None, 0))
        self._record(ev, reads, writes)

    def dma(self, qen, chan, out, in_, reads=(), writes=(), **kw):
        if self.dry:
            return
        if chan not in self.dsem:
            self.dsem[chan] = self.stack.enter_context(self.nc.semaphore("d_" + chan))
            self.dcount[chan] = 0
        self._deps(qen, reads, writes)
        self.dcount[chan] += 1
        fn = lambda e, out=out, in_=in_, kw=kw: e.dma_start(out=out, in_=in_, **kw)
        self.prog[qen].append(("ins", fn, self.dsem[chan], 16))
        ev = (self.dsem[chan], 16 * self.dcount[chan], "dma")
        self._record(ev, reads, writes)

    def fence(self, chan, keys):
        if self.dry:
            return
        ev = (self.dsem[chan], 16 * self.dcount[chan], "dma")
        for k in keys:
            self.last_w[k] = ev

    def finish(self, en):
        if self.dry:
            return
        for chan, sem in self.dsem.items():
            self._wait(en, (sem, 16 * self.dcount[chan], "dma"))

    def replay(self, en, e):
        for item in self.prog[en]:
            if item[0] == "wait":
                e.wait_ge(item[1], item[2])
            else:
                ins = item[1](e)
                if item[2] is not None:
                    ins.then_inc(item[2], item[3])

    def emit(self, block):
        block.sync(lambda e: self.replay("sp", e))
        block.scalar(lambda e: self.replay("act", e))
        block.vector(lambda e: self.replay("dve", e))
        block.gpsimd(lambda e: self.replay("pool", e))
        block.tensor(lambda e: self.replay("pe", e))


class Ring:
    def __init__(self, S, ring_t, plan=None):
        self.S = S
        self.t = ring_t
        self.collect = plan is None
        self.plan = [] if plan is None else plan
        self.next_get = 0
        self.next_issue = 0
        self.released = set()
        if not self.collect:
            self.pump()

    def _view(self, n, shape):
        sl = self.t[:, (n % NSLOT) * SLOT_ELEMS:(n % NSLOT + 1) * SLOT_ELEMS]
        a, b = shape
        return sl[:, :a * b].rearrange("p (a b) -> p a b", a=a)

    def pump(self):
        while self.next_issue < len(self.plan) and (
                self.next_issue < NSLOT or (self.next_issue - NSLOT) in self.released):
            n = self.next_issue
            src, shape = self.plan[n]
            self.S.dma("pool", "ring%d" % (n % NSLOT), self._view(n, shape), src,
                       writes=[("ring", n % NSLOT)])
            self.next_issue += 1

    def get(self, src, shape):
        n = self.next_get
        self.next_get += 1
        if self.collect:
            self.plan.append((src, shape))
            return n, None, ("ring", n % NSLOT)
        assert n < self.next_issue, "weight ring too small for the live set (piece %d)" % n
        return n, self._view(n, shape), ("ring", n % NSLOT)

    def release(self, n):
        if self.collect:
            return
        self.released.add(n)
        self.pump()


class Tile:
    def __init__(self, name, T, slot, x0, segs, y0, ylo):
        self.name = name
        self.T = T
        self.slot = slot
        self.x0 = x0
        self.NS = (T + 127) // 128
        self.rows = [min(128, T - 128 * s) for s in range(self.NS)]
        self.segs = segs
        self.y0 = y0
        self.ylo = ylo


class Seg:
    def __init__(self, col0, ln, gbuf, pbuf, gkey, pkey, tail=None, halo=False, first=False):
        self.col0 = col0
        self.len = ln
        self.gbuf = gbuf
        self.pbuf = pbuf
        self.gkey = gkey
        self.pkey = pkey
        self.tail = tail
        self.halo = halo
        self.first = first


def build_program():
    nc = bass.Bass("TRN2", target_bir_lowering=False)

    def din(name, shape):
        return nc.dram_tensor(name, list(shape), F32, kind="ExternalInput").ap()

    xs_d = din("xs", [NTOK, D])
    mask_d = din("mask", [128, 1])
    invc_d = din("invc", [128, 64])
    sconv_d = din("sconv", [2, 30, DC])
    spool_d = din("spool", [2, 15, DC])
    ffn_d = {}
    for l in (1, 2):
        ffn_d[l] = (din("w%dg" % l, [D, DFF]), din("w%du" % l, [D, DFF]), din("w%dd" % l, [DFF, D]))
    lng_d = {l: din("ln%dg" % l, [1, D]) for l in (1, 2, 3)}
    lnb_d = {l: din("ln%db" % l, [1, D]) for l in (1, 2, 3)}
    win_d = din("win", [D, 3 * DC])
    wgt_d = din("wgt", [D, 2 * D])
    bgt_d = din("bgt", [2 * D])
    cw_d = din("cw", [31, DC])
    cb_d = din("cb", [DC])
    clg_d = din("clg", [DC])
    clb_d = din("clb", [DC])
    wcp_d = din("wcp", [DC, D])
    wpg_d = din("wpg", [4, 128, 128])
    psc_d = din("psc", [DC])
    wpp_d = din("wpp", [DC, D])
    wo_d = din("wo", [D, D])
    y_d = nc.dram_tensor("y", [64 + CHUNK, D], F32, kind="ExternalOutput").ap()
    cs_d = nc.dram_tensor("cs", [3, 32, DC], F32, kind="ExternalOutput").ap()
    pso_d = nc.dram_tensor("pso", [3, 16, DC], F32, kind="ExternalOutput").ap()

    with contextlib.ExitStack() as st:
        def sb(name, shape, dt=F32):
            return st.enter_context(nc.sbuf_tensor("sb_" + name, list(shape), dt))

        R = [sb("R0", [128, 1, D]), sb("R1", [128, 4, D]), sb("R2", [128, 4, D])]
        XT = [sb("XT0", [128, 8, 96], BF16), sb("XT1", [128, 8, 512], BF16), sb("XT2", [128, 8, 512], BF16)]
        ring_t = sb("ring", [128, NSLOT * SLOT_ELEMS], BF16)
        lnA_g, lnA_b = sb("lnA_g", [128, D]), sb("lnA_b", [128, D])
        lng = {1: lnA_g, 2: sb("lng2", [128, D]), 3: lnA_g}
        lnb = {1: lnA_b, 2: sb("lnb2", [128, D]), 3: lnA_b}
        lnk = {1: "A", 2: 2, 3: "A"}
        lngc = {l: sb("lngc%d" % l, [128, 8]) for l in (1, 2)}
        lnbc = {l: sb("lnbc%d" % l, [128, 8]) for l in (1, 2)}
        actdcs = sb("actdcs", [128, 2048], F32)
        dcs = actdcs[:, 0:2048].rearrange("p (c t) -> p c t", c=4)
        actb = [actdcs[:, i * 1024:(i + 1) * 1024].bitcast(BF16).rearrange("p (j t) -> p j t", j=4)
                for i in range(2)]
        sg = [sb("sg%d" % i, [128, 512]) for i in range(2)]
        yn = sg
        gt = [sb("gt%d" % i, [128, 512]) for i in range(4)]
        mixT = sb("mixT", [128, 8, 512], BF16)
        zT = sb("zT", [128, 4, 512], BF16)
        pooled = sb("pooled", [128, 4, 512], BF16)
        pgs = sb("pgs", [128, 4, 512], BF16)
        gluP = [sb("gluP%d" % i, [128, 4, 30 + 512], BF16) for i in range(2)]
        pinP = [sb("pinP%d" % i, [128, 4, 15 + 512]) for i in range(2)]
        gluX = [sb("gluX%d" % i, [128, 4, 30 + 32], BF16) for i in range(3)]
        pinX = [sb("pinX%d" % i, [128, 4, 15 + 32]) for i in range(3)]
        gl32 = [sb("gl32_%d" % i, [128, 4, 32]) for i in range(3)]
        ptmp = [sb("ptmp%d" % i, [128, 528]) for i in range(2)]
        pfx = sb("pfx", [128, 16])
        diag = [sb("diag%d" % i, [128, 31, 128], BF16) for i in range(2)]
        ident = sb("ident", [128, 128])
        identb = sb("identb", [128, 128], BF16)
        cw = sb("cwT", [128, 4, 31])
        cw_raw = ptmp[0]
        sc_raw = [gt[0], gt[1]]
        sp_raw = [gt[2], gt[3]]
        cbv = sb("cbv", [128, 4])
        clgv = sb("clgv", [128, 4])
        clbv = sb("clbv", [128, 4])
        pscv = sb("pscv", [128, 4])
        bgtv = sb("bgtv", [128, 16])
        maskv = sb("maskv", [128, 1])
        invcv = sb("invcv", [128, 64])
        wpg = sb("wpg", [128, 4, 128], BF16)
        stt = sb("stt", [128, 4, 12])
        mv = sb("mv", [128, 4, 2])
        rstd = sb("rstd", [128, 4])
        nmr = sb("nmr", [128, 4])
        stt2 = sb("stt2", [128, 4, 6])
        mv2 = sb("mv2", [128, 4, 2])
        rstd2 = sb("rstd2", [128, 4])
        nmr2 = sb("nmr2", [128, 4])
        ps = st.enter_context(nc.psum_tensor("ps", [128, 8, 512], F32))
        block = st.enter_context(nc.Block())

        segS = [Seg(0, 32, gluX[0], pinX[0], "gX0", "pX0", halo=True),
                Seg(32, 32, gluX[1], pinX[1], "gX1", "pX1", tail=1),
                Seg(64, 32, gluX[2], pinX[2], "gX2", "pX2", tail=2)]
        tS = Tile("S", 96, 0, 0, segS, 0, 32)
        tP = []
        for i in range(4):
            seg = Seg(0, 512, gluP[i % 2], pinP[i % 2], "gP%d" % (i % 2), "pP%d" % (i % 2),
                      tail=(0 if i == 3 else None), first=(i == 0))
            tP.append(Tile("P%d" % i, 512, 1 + i % 2, 96 + 512 * i, [seg], 64 + 512 * i, 0))
        groups = [[tS, tP[0], tP[1]], [tP[2], tP[3]]]
        ffq = [(0, 4), (4, 4), (8, 4), (12, 4), (16, 4), (20, 2)]

        def generate(S, ring):
            state = {"bank": 0, "sgi": 0, "pair": 0, "diag": 0, "reserved": set(), "lnA": 0, "preloaded": set(), "prepped": False, "next_in_slot": {}}

            def bank():
                while True:
                    state["bank"] = (state["bank"] + 1) % 8
                    if state["bank"] not in state["reserved"]:
                        return state["bank"]

            def mm(out, lhsT, rhs, start, stop, reads, writes, inc):
                S.op("pe", lambda e: e.matmul(out, lhsT=lhsT, rhs=rhs, start=start, stop=stop),
                     reads=reads, writes=writes, inc=inc)

            def tr(out, in_, idn, reads, writes, inc):
                S.op("pe", lambda e: e.transpose(out, in_, idn), reads=reads, writes=writes, inc=inc)

            def first_load(t):
                S.dma("sp", "x%d" % t.slot,
                      R[t.slot][:t.T, 0, :] if t.NS == 1 else R[t.slot][:, :, :],
                      xs_d[t.x0:t.x0 + t.T, :] if t.NS == 1 else
                      xs_d[t.x0:t.x0 + t.T, :].rearrange("(s p) d -> p s d", p=128),
                      writes=[("R", t.slot, s_) for s_ in range(t.NS)])
                state["preloaded"].add(t.name)

            for t in groups[0][:2]:
                first_load(t)
            S.op("pool", lambda e: e.memset(ident[:, :], 0.0), writes=["ident"])
            S.op("pool", lambda e: e.affine_select(out=ident[:, :], in_=ident[:, :], compare_op=ALU.not_equal,
                                                  fill=1.0, base=0, pattern=[[-1, 128]], channel_multiplier=1),
                 reads=["ident"], writes=["ident"])
            S.op("dve", lambda e: e.tensor_copy(out=identb[:, :], in_=ident[:, :]), reads=["ident"], writes=["identb"])
            pk = []
            S.dma("sp", "par", lng[2][:, :], lng_d[2].partition_broadcast(128), writes=[("lng", 2)])
            S.dma("sp", "par", lnb[2][:, :], lnb_d[2].partition_broadcast(128), writes=[("lnb", 2)])
            pk += [("lng", 2), ("lnb", 2)]
            for (t_, d_, k_, key) in ((cbv, cb_d, 4, "cbv"), (clgv, clg_d, 4, "clgv"), (clbv, clb_d, 4, "clbv"),
                                      (pscv, psc_d, 4, "pscv"), (bgtv, bgt_d, 16, "bgtv")):
                S.dma("sp", "par", t_[:, :], d_.rearrange("(k p) -> p k", p=128), writes=[key],
                      allow_slow_non_contiguous=True)
                pk.append(key)
            for l in (1, 2):
                S.dma("sp", "par", lngc[l][:, :], lng_d[l].rearrange("o (k p) -> p (o k)", p=128), writes=[("lngc", l)],
                      allow_slow_non_contiguous=True)
                S.dma("sp", "par", lnbc[l][:, :], lnb_d[l].rearrange("o (k p) -> p (o k)", p=128), writes=[("lnbc", l)],
                      allow_slow_non_contiguous=True)
                pk += [("lngc", l), ("lnbc", l)]
            for t in groups[0][2:]:
                first_load(t)
            S.dma("sp", "par", maskv[:, :], mask_d, writes=["maskv"])
            S.dma("sp", "par", invcv[:, :], invc_d, writes=["invcv"])
            S.dma("sp", "par", cw_raw[:31, 0:DC], cw_d, writes=["ptmp"])
            pk += ["maskv", "invcv", "ptmp"]
            for i in range(2):
                S.dma("sp", "par", sc_raw[i][:30, :], sconv_d[i], writes=[("gt", i)])
                S.dma("sp", "par", sp_raw[i][:15, :], spool_d[i], writes=[("gt", 2 + i)])
                pk += [("gt", i), ("gt", 2 + i)]
            S.fence("par", pk)
            S.dma("pool", "parg", wpg[:, :, :], wpg_d.rearrange("g c d -> c g d"), writes=["wpg"])
            S.op("dve", lambda e: e.memset(dcs[:, :, 0:32], 0.0), writes=[("dcs", c) for c in range(4)] + [("act", 0), ("act", 1)])

            def prep_consts():
              if state["prepped"]:
                  return
              state["prepped"] = True
              b = bank()
              for c in range(4):
                  tr(ps[:, b, c * 32:c * 32 + 31], cw_raw[:31, c * 128:(c + 1) * 128], ident[:31, :31],
                     ["ptmp", "ident"], [("ps", b)], c == 3)
              S.op("dve", lambda e, b=b: e.tensor_copy(
                  out=cw[:, :, :], in_=ps[:, b, 0:128].rearrange("p (c k) -> p c k", c=4)[:, :, 0:31]),
                  reads=[("ps", b)], writes=["cw"])
              S.op("dve", lambda e: e.memset(gluX[0][:, :, 0:30], 0.0), writes=[("gX0", "h")])
              S.op("dve", lambda e: e.memset(pinX[0][:, :, 0:15], 0.0), writes=[("pX0", "h")])
              for i in range(2):
                  b = bank()
                  for c in range(4):
                      tr(ps[:, b, c * 32:c * 32 + 30], sc_raw[i][:30, c * 128:(c + 1) * 128], ident[:30, :30],
                         [("gt", i), "ident"], [("ps", b)], c == 3)
                  S.op("dve", lambda e, b=b, i=i: e.tensor_copy(
                      out=gluX[1 + i][:, :, 0:30], in_=ps[:, b, 0:128].rearrange("p (c k) -> p c k", c=4)[:, :, 0:30]),
                      reads=[("ps", b)], writes=[("gX%d" % (1 + i), "h")])
                  b = bank()
                  for c in range(4):
                      tr(ps[:, b, c * 16:c * 16 + 15], sp_raw[i][:15, c * 128:(c + 1) * 128], ident[:15, :15],
                         [("gt", 2 + i), "ident"], [("ps", b)], c == 3)
                  S.op("dve", lambda e, b=b, i=i: e.tensor_copy(
                      out=pinX[1 + i][:, :, 0:15], in_=ps[:, b, 0:64].rearrange("p (c k) -> p c k", c=4)[:, :, 0:15]),
                      reads=[("ps", b)], writes=[("pX%d" % (1 + i), "h")])

            def rkeys(t):
                return [("R", t.slot, s) for s in range(t.NS)]

            def load_x(t):
                if t.name in state["preloaded"]:
                    state["preloaded"].discard(t.name)
                    return
                Rt = R[t.slot]
                if t.NS == 1:
                    S.dma("sp", "x%d" % t.slot, Rt[:t.T, 0, :], xs_d[t.x0:t.x0 + t.T, :], writes=rkeys(t))
                else:
                    S.dma("sp", "x%d" % t.slot, Rt[:, :, :],
                          xs_d[t.x0:t.x0 + t.T, :].rearrange("(s p) d -> p s d", p=128), writes=rkeys(t))

            def store_y(t):
                Rt = R[t.slot]
                if t.NS == 1:
                    S.dma("sp", "y%d" % t.slot, y_d[t.y0:t.y0 + t.T - t.ylo, :], Rt[t.ylo:t.T, 0, :], reads=rkeys(t))
                else:
                    S.dma("sp", "y%d" % t.slot, y_d[t.y0:t.y0 + t.T, :].rearrange("(s p) d -> p s d", p=128),
                          Rt[:, :, :], reads=rkeys(t))

            def load_ln(l):
                if state["lnA"] == l:
                    return
                state["lnA"] = l
                S.dma("sp", "lnAg", lnA_g[:, :], lng_d[l].partition_broadcast(128), writes=[("lng", "A")])
                S.dma("sp", "lnAb", lnA_b[:, :], lnb_d[l].partition_broadcast(128), writes=[("lnb", "A")])

            def transposes(t, aff=None):
                Rt, Xt = R[t.slot], XT[t.slot]
                for k in range(8):
                    b = bank()
                    for s in range(t.NS):
                        r = t.rows[s]
                        tr(ps[:, b, s * 128:s * 128 + r], Rt[:r, s, k * 128:(k + 1) * 128], ident[:r, :r],
                           [("R", t.slot, s), "ident"], [("ps", b)], s == t.NS - 1)
                    if aff is None:
                        S.op("act", lambda e, b=b, k=k: e.activation(out=Xt[:, k, :t.T], in_=ps[:, b, :t.T], func=AF.Copy),
                             reads=[("ps", b)], writes=[("XT", t.slot, k)])
                    else:
                        S.op("act", lambda e, b=b, k=k: e.activation(out=Xt[:, k, :t.T], in_=ps[:, b, :t.T], func=AF.Identity,
                                                                   bias=lnbc[aff][:, k:k + 1], scale=lngc[aff][:, k:k + 1]),
                             reads=[("ps", b), ("lngc", aff), ("lnbc", aff)], writes=[("XT", t.slot, k)])

            def ln_affine(t, l):
                Rt = R[t.slot]
                for s in range(t.NS):
                    r = t.rows[s]
                    key = ("R", t.slot, s)
                    S.op("dve", lambda e, s=s, r=r: e.tensor_tensor(out=Rt[:r, s, :], in0=Rt[:r, s, :], in1=lng[l][:r, :],
                                                                  op=ALU.mult), reads=[key, ("lng", lnk[l])], writes=[key])
                    S.op("dve", lambda e, s=s, r=r: e.tensor_tensor(out=Rt[:r, s, :], in0=Rt[:r, s, :], in1=lnb[l][:r, :],
                                                                  op=ALU.add), reads=[key, ("lnb", lnk[l])], writes=[key])

            def ln_part1(t, eps):
                Rt = R[t.slot]
                for s in range(t.NS):
                    r = t.rows[s]
                    for h in range(2):
                        S.op("dve", lambda e, s=s, r=r, h=h: e.bn_stats(stt[:r, s, h * 6:(h + 1) * 6],
                                                                      Rt[:r, s, h * 512:(h + 1) * 512]),
                             reads=[("R", t.slot, s)], writes=[("stt", s)])
                    S.op("dve", lambda e, s=s, r=r: e.bn_aggr(mv[:r, s, :], stt[:r, s, :]),
                         reads=[("stt", s)], writes=["mv"])
                r0 = t.rows[0] if t.NS == 1 else 128
                ns = t.NS
                S.op("dve", lambda e: e.tensor_scalar(out=rstd[:r0, :ns], in0=mv[:r0, :ns, 1], scalar1=eps,
                                                      scalar2=None, op0=ALU.add), reads=["mv"], writes=["rstd"])
                S.op("dve", lambda e: e.reciprocal(out=rstd[:r0, :ns], in_=rstd[:r0, :ns]), reads=["rstd"], writes=["rstd"])

            def ln_part2(t, l, affine):
                Rt = R[t.slot]
                r0 = t.rows[0] if t.NS == 1 else 128
                ns = t.NS
                S.op("act", lambda e: e.activation(out=rstd[:r0, :ns], in_=rstd[:r0, :ns], func=AF.Sqrt),
                     reads=["rstd"], writes=["rstd"])
                S.op("dve", lambda e: e.scalar_tensor_tensor(out=nmr[:r0, :ns], in0=mv[:r0, :ns, 0], scalar=-1.0,
                                                             in1=rstd[:r0, :ns], op0=ALU.mult, op1=ALU.mult),
                     reads=["mv", "rstd"], writes=["nmr"])
                for s in range(t.NS):
                    r = t.rows[s]
                    key = ("R", t.slot, s)
                    S.op("act", lambda e, s=s, r=r: e.activation(out=Rt[:r, s, :], in_=Rt[:r, s, :], func=AF.Identity,
                                                               bias=nmr[:r, s:s + 1], scale=rstd[:r, s:s + 1]),
                         reads=[key, "rstd", "nmr"], writes=[key])
                if affine:
                    ln_affine(t, l)

            def layer_norm(t, l, eps, affine=True):
                ln_part1(t, eps)
                ln_part2(t, l, affine)

            def ffn_gen(tiles, l, final, first_tr=None, head_skew=False):
                wg_d, wu_d, wd_d = ffn_d[l]
                nq = len(ffq)
                if head_skew and len(tiles) == 2:
                    head = [(q, t) for t in tiles for q in (0, 1)]
                    mid0 = 2
                else:
                    head = []
                    mid0 = 0
                pairs = (head + [(q, t) for q in range(mid0, nq - 2) for t in tiles]
                         + [(q, t) for t in tiles for q in (nq - 2, nq - 1)])
                pieces = {}

                def gu(q, t):
                    j0, nj = ffq[q]
                    if q not in pieces:
                        gp, up = [], []
                        for h in range((nj + 1) // 2):
                            c0 = (j0 + 2 * h) * 128
                            w = min(256, (j0 + nj) * 128 - c0)
                            gp.append(ring.get(wg_d[:, c0:c0 + w].rearrange("(k p) f -> p k f", p=128), (8, w)))
                            up.append(ring.get(wu_d[:, c0:c0 + w].rearrange("(k p) f -> p k f", p=128), (8, w)))
                        pieces[q] = {"g": gp, "u": up}
                    P = pieces[q]
                    if q == 0 and first_tr == "x":
                        transposes(t)
                    if q == 0 and first_tr == "h2":
                        transposes(t, aff=2)
                        ln_affine(t, 2)
                    Xt = XT[t.slot]
                    ab = state["pair"] % 2
                    state["pair"] += 1
                    for jj in range(nj):
                        bg, bu = bank(), bank()
                        for (bnk, pl) in ((bg, P["g"]), (bu, P["u"])):
                            n, v, key = pl[jj // 2]
                            for k in range(8):
                                mm(ps[:, bnk, :t.T], None if v is None else v[:, k, (jj % 2) * 128:(jj % 2 + 1) * 128],
                                   Xt[:, k, :t.T], k == 0, k == 7, [key, ("XT", t.slot, k)], [("ps", bnk)], k == 7)
                        si = state["sgi"] % 2
                        state["sgi"] += 1
                        S.op("act", lambda e, bg=bg, si=si: e.activation(out=sg[si][:, :t.T], in_=ps[:, bg, :t.T], func=AF.Silu),
                             reads=[("ps", bg)], writes=[("sg", si)])
                        S.op("dve", lambda e, bu=bu, si=si, jj=jj, ab=ab: e.tensor_tensor(
                            out=actb[ab][:, jj, :t.T], in0=ps[:, bu, :t.T], in1=sg[si][:, :t.T], op=ALU.mult),
                            reads=[("ps", bu), ("sg", si)], writes=[("act", ab)])
                    if t is tiles[-1]:
                        for pl in (P["g"], P["u"]):
                            for (n, v, key) in pl:
                                ring.release(n)
                    return ab

                def down(q, t, ab):
                    j0, nj = ffq[q]
                    P = pieces[q]
                    if "d" not in P:
                        P["d"] = [ring.get(wd_d[j0 * 128:(j0 + nj) * 128, f * 512:(f + 1) * 512]
                                           .rearrange("(j p) n -> p j n", p=128), (nj, 512)) for f in range(2)]
                    Rt = R[t.slot]
                    for f in range(2):
                        n, v, key = P["d"][f]
                        for s in range(t.NS):
                            r = t.rows[s]
                            b = bank()
                            for jj in range(nj):
                                mm(ps[:r, b, :512], actb[ab][:, jj, s * 128:s * 128 + r],
                                   None if v is None else v[:, jj, :], jj == 0, jj == nj - 1,
                                   [key, ("act", ab)], [("ps", b)], jj == nj - 1)
                            rk = ("R", t.slot, s)
                            S.op("dve", lambda e, b=b, s=s, r=r, f=f: e.scalar_tensor_tensor(
                                out=Rt[:r, s, f * 512:(f + 1) * 512], in0=ps[:r, b, :512], scalar=C_FFN,
                                in1=Rt[:r, s, f * 512:(f + 1) * 512], op0=ALU.mult, op1=ALU.add),
                                reads=[("ps", b), rk], writes=[rk])
                    if t is tiles[-1]:
                        for (n, v, key) in P["d"]:
                            ring.release(n)

                def ln_finish(t):
                    if not final:
                        ln_part2(t, 1, False)
                    else:
                        Rt = R[t.slot]
                        r0 = t.rows[0] if t.NS == 1 else 128
                        ns = t.NS
                        S.op("act", lambda e: e.activation(out=rstd[:r0, :ns], in_=rstd[:r0, :ns], func=AF.Sqrt),
                             reads=["rstd"], writes=["rstd"])
                        S.op("dve", lambda e: e.scalar_tensor_tensor(out=nmr[:r0, :ns], in0=mv[:r0, :ns, 0], scalar=-1.0,
                                                                     in1=rstd[:r0, :ns], op0=ALU.mult, op1=ALU.mult),
                             reads=["mv", "rstd"], writes=["nmr"])
                        for s in range(t.NS):
                            r = t.rows[s]
                            key = ("R", t.slot, s)
                            S.op("act", lambda e, s=s, r=r: e.activation(out=Rt[:r, s, :], in_=Rt[:r, s, :], func=AF.Identity,
                                                                       bias=nmr[:r, s:s + 1], scale=rstd[:r, s:s + 1]),
                                 reads=[key, "rstd", "nmr"], writes=[key])
                            S.op("dve", lambda e, s=s, r=r: e.tensor_tensor(out=Rt[:r, s, :], in0=Rt[:r, s, :], in1=lng[3][:r, :],
                                                                          op=ALU.mult), reads=[key, ("lng", "A")], writes=[key])
                            S.op("dve", lambda e, s=s, r=r: e.tensor_tensor(out=Rt[:r, s, :], in0=Rt[:r, s, :], in1=lnb[3][:r, :],
                                                                          op=ALU.add), reads=[key, ("lnb", "A")], writes=[key])
                            if t.NS == 1:
                                S.dma("sp", "y%d_%d" % (t.slot, s), y_d[t.y0:t.y0 + t.T - t.ylo, :], Rt[t.ylo:t.T, 0, :],
                                      reads=[key])
                            else:
                                S.dma("sp", "y%d_%d" % (t.slot, s), y_d[t.y0 + s * 128:t.y0 + s * 128 + r, :], Rt[:r, s, :],
                                      reads=[key])
                    if final:
                        nt = state["next_in_slot"].get(t.slot)
                        if nt is not None:
                            state["next_in_slot"][t.slot] = None
                            load_x(nt)
                            state["preloaded"].add(nt.name)

                pend = None
                ln_pend = None
                abs_ = {}
                abs_[0] = gu(*pairs[0])
                yield ("prologue", pairs[0])
                for i in range(len(pairs)):
                    if i + 1 < len(pairs):
                        abs_[i + 1] = gu(*pairs[i + 1])
                    down(pairs[i][0], pairs[i][1], abs_[i])
                    if pend is not None:
                        transposes(pend, aff=1)
                        yield ("tr", pend)
                        pend = None
                    if ln_pend is not None:
                        ln_finish(ln_pend)
                        if not final:
                            pend = ln_pend
                        ln_pend = None
                    if pairs[i][0] == len(ffq) - 1:
                        ln_part1(pairs[i][1], EPS_FFN)
                        ln_pend = pairs[i][1]
                    yield ("step", pairs[i])
                if ln_pend is not None:
                    if final:
                        yield ("before_last_ln", ln_pend)
                    ln_finish(ln_pend)
                    if not final:
                        if pend is not None:
                            transposes(pend, aff=1)
                        pend = ln_pend
                if pend is not None:
                    yield ("defer_tr", pend)

            def build_diag(c):
                i = state["diag"] % 2
                state["diag"] += 1
                i0 = identb[:, :].unsqueeze(1).broadcast_to([128, 31, 128])
                i1 = cw[:, c, :].unsqueeze(2).broadcast_to([128, 31, 128])
                S.op("dve", lambda e, i=i: e.tensor_tensor(out=diag[i][:, :, :], in0=i0, in1=i1, op=ALU.mult),
                     reads=["identb", "cw"], writes=[("diag", i)])
                return i

            def stage_a(cx):
                t = cx["t"]
                T = t.T
                Xt = XT[t.slot]
                xk = [("XT", t.slot, k) for k in range(8)]
                wi = {}
                prep_consts()
                dnext = build_diag(0)

                def win_piece(i):
                    if i not in wi:
                        wi[i] = ring.get(win_d[:, i * 256:(i + 1) * 256].rearrange("(k p) f -> p k f", p=128), (8, 256))
                    return wi[i]

                def proj_in(m):
                    n, v, key = win_piece(m // 2)
                    b = bank()
                    for k in range(8):
                        mm(ps[:, b, :T], None if v is None else v[:, k, (m % 2) * 128:(m % 2 + 1) * 128],
                           Xt[:, k, :T], k == 0, k == 7, [key, xk[k]], [("ps", b)], k == 7)
                    return b

                for c in range(4):
                    ba = proj_in(c)
                    bb = proj_in(4 + c)
                    si = state["sgi"] % 2
                    state["sgi"] += 1
                    S.op("act", lambda e, bb=bb, si=si: e.activation(out=sg[si][:, :T], in_=ps[:, bb, :T], func=AF.Sigmoid),
                         reads=[("ps", bb)], writes=[("sg", si)])
                    for sgm in t.segs:
                        c0, ln = sgm.col0, sgm.len
                        S.op("dve", lambda e, ba=ba, si=si, c=c, sgm=sgm, c0=c0, ln=ln: e.tensor_tensor(
                            out=sgm.gbuf[:, c, 30:30 + ln], in0=ps[:, ba, c0:c0 + ln], in1=sg[si][:, c0:c0 + ln], op=ALU.mult),
                            reads=[("ps", ba), ("sg", si)], writes=[(sgm.gkey, c)])
                        if sgm.tail is not None:
                            S.op("dve", lambda e, ba=ba, si=si, c=c, sgm=sgm, c0=c0, ln=ln: e.tensor_tensor(
                                out=gl32[sgm.tail][:, c, :], in0=ps[:, ba, c0 + ln - 32:c0 + ln],
                                in1=sg[si][:, c0 + ln - 32:c0 + ln], op=ALU.mult),
                                reads=[("ps", ba), ("sg", si)], writes=[("gl32", sgm.tail)])
                        if sgm.halo:
                            S.op("dve", lambda e, c=c, sgm=sgm: e.tensor_scalar(
                                out=gluP[0][:, c, 0:30], in0=sgm.gbuf[:, c, 32:62], scalar1=maskv[:, 0:1], scalar2=None,
                                op0=ALU.mult), reads=[(sgm.gkey, c), "maskv"], writes=[("gP0", "h")])
                    if c % 2 == 1:
                        ring.release(win_piece(c // 2)[0])
                        ring.release(win_piece(2 + c // 2)[0])
                yield "A1"
                for c in range(4):
                    b = proj_in(8 + c)
                    for sgm in t.segs:
                        c0, ln = sgm.col0, sgm.len
                        S.op("act", lambda e, b=b, c=c, sgm=sgm, c0=c0, ln=ln: e.activation(
                            out=sgm.pbuf[:, c, 15:15 + ln], in_=ps[:, b, c0:c0 + ln], func=AF.Copy),
                            reads=[("ps", b)], writes=[(sgm.pkey, c)])
                        if sgm.halo:
                            S.op("dve", lambda e, c=c, sgm=sgm: e.tensor_scalar(
                                out=pinP[0][:, c, 0:15], in0=sgm.pbuf[:, c, 32:47], scalar1=maskv[:, 0:1], scalar2=None,
                                op0=ALU.mult), reads=[(sgm.pkey, c), "maskv"], writes=[("pP0", "h")])
                    if c % 2 == 1:
                        ring.release(win_piece(4 + c // 2)[0])
                if any(sgm.halo for sgm in t.segs):
                    S.op("dve", lambda e: e.memset(dcs[:, :, 0:32], 0.0),
                         writes=[("dcs", c) for c in range(4)] + [("act", 0), ("act", 1)])
                for c in range(4):
                    di = dnext
                    b = bank()
                    csegs = [sgm for sgm in t.segs if not sgm.halo]
                    lo_c = csegs[0].col0
                    for si_, sgm in enumerate(csegs):
                        c0, ln = sgm.col0, sgm.len
                        for k in range(31):
                            mm(ps[:, b, c0:c0 + ln], diag[di][:, k, :], sgm.gbuf[:, c, k:k + ln], k == 0, k == 30,
                               [("diag", di), (sgm.gkey, c), (sgm.gkey, "h")], [("ps", b)],
                               k == 30 and si_ == len(csegs) - 1)
                    if c < 3:
                        dnext = build_diag(c + 1)
                    S.op("act", lambda e, b=b, c=c, lo_c=lo_c: e.activation(out=dcs[:, c, lo_c:T], in_=ps[:, b, lo_c:T],
                                                                          func=AF.Identity, bias=cbv[:, c:c + 1]),
                         reads=[("ps", b), "cbv"], writes=[("dcs", c), ("act", c // 2)])
                ln_affine(t, 1)

            def stage_b1(cx):
                t = cx["t"]
                fb = []
                for s in range(t.NS):
                    r = t.rows[s]
                    b = bank()
                    fb.append(b)
                    for c in range(4):
                        tr(ps[:r, b, c * 128:(c + 1) * 128], dcs[:, c, s * 128:s * 128 + r], ident[:, :],
                           [("dcs", c), ("act", c // 2), "ident"], [("ps", b)], c == 3)
                    S.op("dve", lambda e, s=s, r=r, b=b: e.bn_stats(stt2[:r, s, 0:6], ps[:r, b, :512]),
                         reads=[("ps", b)], writes=[("stt2", s)])
                    S.op("dve", lambda e, s=s, r=r: e.bn_aggr(mv2[:r, s, :], stt2[:r, s, 0:6]),
                         reads=[("stt2", s)], writes=["mv2"])
                cx["fb"] = fb
                state["reserved"] = set(fb)
                r0 = t.rows[0] if t.NS == 1 else 128
                ns = t.NS
                S.op("dve", lambda e: e.tensor_scalar(out=rstd2[:r0, :ns], in0=mv2[:r0, :ns, 1], scalar1=EPS, scalar2=None,
                                                      op0=ALU.add), reads=["mv2"], writes=["rstd2"])
                S.op("dve", lambda e: e.reciprocal(out=rstd2[:r0, :ns], in_=rstd2[:r0, :ns]), reads=["rstd2"], writes=["rstd2"])
                yield "B1a"
                if cx["nxt"] is not None:
                    sg0 = t.segs[0]
                    nb = cx["nxt"].segs[0]
                    S.op("dve", lambda e: e.tensor_copy(out=nb.gbuf[:, :, 0:30], in_=sg0.gbuf[:, :, 512:542]),
                         reads=[(sg0.gkey, c) for c in range(4)], writes=[(nb.gkey, "h")])
                    S.op("dve", lambda e: e.tensor_copy(out=nb.pbuf[:, :, 0:15], in_=sg0.pbuf[:, :, 512:527]),
                         reads=[(sg0.pkey, c) for c in range(4)], writes=[(nb.pkey, "h")])
                for sgm in t.segs:
                    c0, ln = sgm.col0, sgm.len
                    L = 15 + ln
                    for c in range(4):
                        E = sgm.pbuf[:, c, :]
                        src = E
                        off = 1
                        lo = 1
                        for step in range(c + 1):
                            dst = ptmp[step % 2]
                            S.op("dve", lambda e, dst=dst, src=src, lo=lo, off=off, L=L: e.tensor_tensor(
                                out=dst[:, lo:L], in0=src[:, lo:L], in1=src[:, lo - off:L - off], op=ALU.add),
                                reads=[(sgm.pkey, c), (sgm.pkey, "h"), "ptmp"], writes=["ptmp"])
                            src = dst
                            off *= 2
                            lo = 2 * off - 1
                        w = POOL_W[c]
                        S.op("dve", lambda e, src=src, w=w, L=L: e.tensor_scalar(
                            out=src[:, 15:L], in0=src[:, 15:L], scalar1=1.0 / w, scalar2=None, op0=ALU.mult),
                            reads=["ptmp"], writes=["ptmp"])
                        S.op("dve", lambda e, src=src, E=E, c=c, c0=c0, ln=ln, L=L: e.tensor_tensor(
                            out=pooled[:, c, c0:c0 + ln], in0=src[:, 15:L], in1=E[:, 15:L], op=ALU.subtract),
                            reads=["ptmp", (sgm.pkey, c)], writes=[("pooled", c)])
                        if sgm.first:
                            S.op("dve", lambda e, src=src, c=c: e.tensor_tensor(
                                out=pfx[:, :], in0=src[:, 15:31], in1=invcv[:, c * 16:(c + 1) * 16], op=ALU.mult),
                                reads=["ptmp", "invcv"], writes=["pfx"])
                            S.op("dve", lambda e, E=E, c=c, c0=c0: e.tensor_tensor(
                                out=pooled[:, c, c0:c0 + 16], in0=pfx[:, :], in1=E[:, 15:31], op=ALU.subtract),
                                reads=["pfx", (sgm.pkey, c)], writes=[("pooled", c)])
                        yield "pool"
                for sgm in t.segs:
                    if sgm.tail is None:
                        continue
                    i = sgm.tail
                    L = 15 + sgm.len
                    j0 = 2 if i == 2 else 0
                    so_c, so_p = gt[j0][:32, :], gt[j0 + 1][:16, :]
                    kc, kp = ("gt", j0), ("gt", j0 + 1)
                    b = bank()
                    for c in range(4):
                        tr(ps[:32, b, c * 128:(c + 1) * 128], gl32[i][:, c, :], ident[:, :],
                           [("gl32", i), "ident"], [("ps", b)], c == 3)
                    S.op("act", lambda e, b=b, so_c=so_c: e.activation(out=so_c, in_=ps[:32, b, :512], func=AF.Copy),
                         reads=[("ps", b)], writes=[kc])
                    S.dma("sp", "so_c%d" % j0, cs_d[i], so_c, reads=[kc])
                    b = bank()
                    for c in range(4):
                        tr(ps[:16, b, c * 128:(c + 1) * 128], sgm.pbuf[:, c, L - 16:L], ident[:, :],
                           [(sgm.pkey, c), "ident"], [("ps", b)], c == 3)
                    S.op("act", lambda e, b=b, so_p=so_p: e.activation(out=so_p, in_=ps[:16, b, :512], func=AF.Copy),
                         reads=[("ps", b)], writes=[kp])
                    S.dma("sp", "so_p%d" % j0, pso_d[i], so_p, reads=[kp])

            def b2_pre(cx):
                if cx.get("b2pre"):
                    return
                cx["b2pre"] = True
                t = cx["t"]
                r0 = t.rows[0] if t.NS == 1 else 128
                ns = t.NS
                S.op("act", lambda e: e.activation(out=rstd2[:r0, :ns], in_=rstd2[:r0, :ns], func=AF.Sqrt),
                     reads=["rstd2"], writes=["rstd2"])
                S.op("dve", lambda e: e.scalar_tensor_tensor(out=nmr2[:r0, :ns], in0=mv2[:r0, :ns, 0], scalar=-1.0,
                                                             in1=rstd2[:r0, :ns], op0=ALU.mult, op1=ALU.mult),
                     reads=["mv2", "rstd2"], writes=["nmr2"])

            def stage_b2(cx):
                t = cx["t"]
                T = t.T
                fb = cx["fb"]
                b2_pre(cx)
                zb = [bank() for c in range(4)]
                for s in range(t.NS):
                    r = t.rows[s]
                    yi = s % 2
                    S.op("act", lambda e, s=s, r=r, yi=yi: e.activation(out=yn[yi][:r, :], in_=ps[:r, fb[s], :512],
                                                                      func=AF.Identity, bias=nmr2[:r, s:s + 1],
                                                                      scale=rstd2[:r, s:s + 1]),
                         reads=[("ps", fb[s]), "rstd2", "nmr2"], writes=[("sg", yi)])
                    for c in range(4):
                        tr(ps[:, zb[c], s * 128:s * 128 + r], yn[yi][:r, c * 128:(c + 1) * 128], ident[:r, :r],
                           [("sg", yi), "ident"], [("ps", zb[c])], True)
                state["reserved"] = set()
                for c in range(4):
                    S.op("act", lambda e, c=c: e.activation(out=zT[:, c, :T], in_=ps[:, zb[c], :T], func=AF.Silu,
                                                          bias=clbv[:, c:c + 1], scale=clgv[:, c:c + 1]),
                         reads=[("ps", zb[c]), "clgv", "clbv"], writes=[("zT", c)])
                for c in range(4):
                    b = bank()
                    mm(ps[:, b, :T], wpg[:, c, :], pooled[:, c, :T], True, True, ["wpg", ("pooled", c)], [("ps", b)], True)
                    S.op("act", lambda e, b=b, c=c: e.activation(out=pgs[:, c, :T], in_=ps[:, b, :T], func=AF.Identity,
                                                               scale=pscv[:, c:c + 1]),
                         reads=[("ps", b), "pscv"], writes=[("pgs", c)])

            def stage_c(cx):
                t = cx["t"]
                T = t.T
                Xt, Rt = XT[t.slot], R[t.slot]
                xk = [("XT", t.slot, k) for k in range(8)]
                wgp = {}
                prj = {}

                def gate_piece(i):
                    if i not in wgp:
                        wgp[i] = ring.get(wgt_d[:, i * 256:(i + 1) * 256].rearrange("(k p) f -> p k f", p=128), (8, 256))
                    return wgp[i]

                def proj_piece(nm, wd, f):
                    if (nm, f) not in prj:
                        prj[(nm, f)] = ring.get(wd[:, f * 512:(f + 1) * 512].rearrange("(c p) n -> p c n", p=128), (4, 512))
                    return prj[(nm, f)]

                for m in range(8):
                    g0, g1 = gt[(m % 2) * 2], gt[(m % 2) * 2 + 1]
                    k0, k1 = ("gt", (m % 2) * 2), ("gt", (m % 2) * 2 + 1)
                    for (gi, gtile, gkey, nm, wd, rhs_t, rkey) in ((0, g0, k0, "cp", wcp_d, zT, "zT"),
                                                                 (1, g1, k1, "pp", wpp_d, pgs, "pgs")):
                        mg = m + 8 * gi
                        n, v, key = gate_piece(mg // 2)
                        b = bank()
                        for k in range(8):
                            mm(ps[:, b, :T], None if v is None else v[:, k, (mg % 2) * 128:(mg % 2 + 1) * 128],
                               Xt[:, k, :T], k == 0, k == 7, [key, xk[k]], [("ps", b)], k == 7)
                        S.op("act", lambda e, b=b, gtile=gtile, mg=mg: e.activation(
                            out=gtile[:, :T], in_=ps[:, b, :T], func=AF.Sigmoid, bias=bgtv[:, mg:mg + 1]),
                            reads=[("ps", b), "bgtv"], writes=[gkey])
                        n2, v2, key2 = proj_piece(nm, wd, m // 4)
                        b2 = bank()
                        for c in range(4):
                            mm(ps[:, b2, :T], None if v2 is None else v2[:, c, (m % 4) * 128:(m % 4 + 1) * 128],
                               rhs_t[:, c, :T], c == 0, c == 3, [key2, (rkey, c)], [("ps", b2)], c == 3)
                        S.op("dve", lambda e, b2=b2, gtile=gtile: e.tensor_tensor(
                            out=gtile[:, :T], in0=gtile[:, :T], in1=ps[:, b2, :T], op=ALU.mult),
                            reads=[("ps", b2), gkey], writes=[gkey])
                    S.op("dve", lambda e, m=m, g0=g0, g1=g1: e.tensor_tensor(
                        out=mixT[:, m, :T], in0=g0[:, :T], in1=g1[:, :T], op=ALU.add),
                        reads=[k0, k1], writes=[("mixT", m)])
                    if m % 2 == 1:
                        ring.release(gate_piece(m // 2)[0])
                        ring.release(gate_piece(4 + m // 2)[0])
                    if m % 4 == 3:
                        ring.release(proj_piece("cp", wcp_d, m // 4)[0])
                        ring.release(proj_piece("pp", wpp_d, m // 4)[0])
                    yield "m"
                for f in range(2):
                    wop = [ring.get(wo_d[kh * 512:(kh + 1) * 512, f * 512:(f + 1) * 512]
                                    .rearrange("(k p) n -> p k n", p=128), (4, 512)) for kh in range(2)]
                    for s in range(t.NS):
                        r = t.rows[s]
                        b = bank()
                        for k in range(8):
                            n, v, key = wop[k // 4]
                            mm(ps[:r, b, :512], mixT[:, k, s * 128:s * 128 + r], None if v is None else v[:, k % 4, :],
                               k == 0, k == 7, [key, ("mixT", k)], [("ps", b)], k == 7)
                        rk = ("R", t.slot, s)
                        S.op("dve", lambda e, b=b, s=s, r=r, f=f: e.scalar_tensor_tensor(
                            out=Rt[:r, s, f * 512:(f + 1) * 512], in0=Rt[:r, s, f * 512:(f + 1) * 512], scalar=ALPHA,
                            in1=ps[:r, b, :512], op0=ALU.mult, op1=ALU.add), reads=[("ps", b), rk], writes=[rk])
                    for (n, v, key) in wop:
                        ring.release(n)
                ln_part1(t, EPS)

            def run_group(tiles, next_tiles):
                cxs = []
                for t in tiles:
                    nxt = None
                    if t.name.startswith("P") and t.name != "P3":
                        nxt = tP[int(t.name[1]) + 1]
                    cxs.append({"t": t, "nxt": nxt})
                load_x(tiles[0])

                def fin(g):
                    for _ in g:
                        pass

                g1 = ffn_gen(tiles, 1, False, first_tr="x", head_skew=True)
                next(g1)
                yield "ffn1_prologue"
                for t in tiles[1:]:
                    load_x(t)
                load_ln(1)
                deferred = None
                for tag in g1:
                    if tag[0] == "defer_tr":
                        deferred = tag[1]
                n = len(cxs)
                if cxs[0]["t"] is deferred:
                    transposes(deferred, aff=1)
                fin(stage_a(cxs[0]))
                ln2_pend = None
                for i in range(n):
                    if i + 1 < n and cxs[i + 1]["t"] is deferred:
                        transposes(deferred, aff=1)
                    gb = stage_b1(cxs[i])
                    ga = stage_a(cxs[i + 1]) if i + 1 < n else iter(())
                    next(gb, None)
                    next(ga, None)
                    if i + 1 < n or i == 0:
                        fin(gb)
                    fin(ga)
                    if i > 0:
                        if ln2_pend is not None:
                            ln_part2(ln2_pend, 2, False)
                        gc_ = stage_c(cxs[i - 1])
                        next(gc_, None)
                        b2_pre(cxs[i])
                        while True:
                            r1 = next(gb, "END")
                            r2 = next(gc_, "END")
                            if r2 != "END":
                                r2 = next(gc_, "END")
                            if r1 == "END" and r2 == "END":
                                break
                        ln2_pend = cxs[i - 1]["t"]
                    stage_b2(cxs[i])
                if ln2_pend is not None:
                    ln_part2(ln2_pend, 2, False)
                fin(stage_c(cxs[n - 1]))
                load_ln(3)
                for nt in next_tiles:
                    state["next_in_slot"][nt.slot] = nt
                order2 = [t for t in tiles[:-1] if t.T == 512] + [t for t in tiles[:-1] if t.T != 512] + [tiles[-1]]
                g2 = ffn_gen(order2, 2, True, first_tr="h2")
                next(g2)
                if len(order2) >= 3:
                    next(g2)
                ln_part2(tiles[-1], 2, False)
                for tag in g2:
                    if tag[0] == "before_last_ln":
                        yield "ffn2_before_last_ln"

            g = run_group(groups[0], groups[1] if len(groups) > 1 else [])
            next(g)
            for gi in range(len(groups)):
                next(g)
                gn = None
                if gi + 1 < len(groups):
                    gn = run_group(groups[gi + 1], groups[gi + 2] if gi + 2 < len(groups) else [])
                    next(gn)
                for _ in g:
                    pass
                g = gn
            S.finish("sp")

        dryS = Sched(nc, st, dry=True)
        plan_ring = Ring(dryS, ring_t, plan=None)
        generate(dryS, plan_ring)
        S = Sched(nc, st)
        ring = Ring(S, ring_t, plan=plan_ring.plan)
        generate(S, ring)
        assert ring.next_get == len(ring.plan) == ring.next_issue, (ring.next_get, len(ring.plan), ring.next_issue)
        S.emit(block)
        build_program.stats = (dict(S.nins), {e: len(S.prog[e]) for e in S.ENGS}, len(ring.plan))
    return nc


_CACHE = {}


def kernel(x_prompt, x_sample, state_conv, state_pool,
           w_ffn1_gate, w_ffn1_up, w_ffn1_down, ln1_g, ln1_b,
           w_in, w_gate, b_gate, conv_w, conv_b, conv_ln_g, conv_ln_b, w_conv_proj,
           w_pool_group, pool_scale, w_pool_proj, w_out, ln2_g, ln2_b,
           w_ffn2_gate, w_ffn2_up, w_ffn2_down, ln3_g, ln3_b):
    f = lambda a: np.ascontiguousarray(np.asarray(a, dtype=np.float32))
    x_prompt, x_sample = f(x_prompt), f(x_sample)
    state_conv, state_pool = f(state_conv), f(state_pool)
    shared = {
        "w1g": f(w_ffn1_gate)[0], "w1u": f(w_ffn1_up)[0], "w1d": f(w_ffn1_down)[0],
        "w2g": f(w_ffn2_gate)[0], "w2u": f(w_ffn2_up)[0], "w2d": f(w_ffn2_down)[0],
        "ln1g": f(ln1_g), "ln1b": f(ln1_b), "ln2g": f(ln2_g), "ln2b": f(ln2_b), "ln3g": f(ln3_g), "ln3b": f(ln3_b),
        "win": f(w_in)[0], "wgt": f(w_gate)[0], "bgt": f(b_gate)[0], "cw": f(conv_w)[0], "cb": f(conv_b)[0],
        "clg": f(conv_ln_g)[0], "clb": f(conv_ln_b)[0], "wcp": f(w_conv_proj)[0], "wpg": f(w_pool_group)[0],
        "psc": f(pool_scale)[0], "wpp": f(w_pool_proj)[0], "wo": f(w_out)[0],
    }
    in_maps = []
    for c in range(NCORES):
        b, ch = c // 4, c % 4
        t0 = ch * CHUNK
        xs = np.zeros((NTOK, D), np.float32)
        if ch > 0:
            xs[0:HALO] = x_prompt[b, t0 - HALO:t0]
        xs[32:64] = x_sample[2 * c]
        xs[64:96] = x_sample[2 * c + 1]
        xs[96:] = x_prompt[b, t0:t0 + CHUNK]
        mask = np.full((128, 1), 1.0 if ch > 0 else 0.0, np.float32)
        invc = np.zeros((128, 64), np.float32)
        for g, w in enumerate(POOL_W):
            for p in range(16):
                cnt = min(p + 1, w) if ch == 0 else w
                invc[:, g * 16 + p] = float(w) / cnt
        m = dict(shared)
        m.update({"xs": xs, "mask": mask, "invc": invc,
                  "sconv": np.ascontiguousarray(state_conv[0, 2 * c:2 * c + 2]),
                  "spool": np.ascontiguousarray(state_pool[0, 2 * c:2 * c + 2])})
        in_maps.append(m)
    if "nc" not in _CACHE:
        _CACHE["nc"] = build_program()
    res = run_bass_kernel_spmd(_CACHE["nc"], in_maps, core_ids=list(range(NCORES)))
    y_prompt = np.zeros((2, SEQ, D), np.float32)
    y_sample = np.zeros((16, DSEQ, D), np.float32)
    ncp = np.zeros((1, 2, 30, DC), np.float32)
    ncs = np.zeros((1, 16, 30, DC), np.float32)
    npp = np.zeros((1, 2, 15, DC), np.float32)
    nps = np.zeros((1, 16, 15, DC), np.float32)
    for c in range(NCORES):
        r = res.results[c]
        b, ch = c // 4, c % 4
        y = np.asarray(r["y"])
        cs = np.asarray(r["cs"])
        pso = np.asarray(r["pso"])
        y_prompt[b, ch * CHUNK:(ch + 1) * CHUNK] = y[64:]
        y_sample[2 * c] = y[0:32]
        y_sample[2 * c + 1] = y[32:64]
        ncs[0, 2 * c] = cs[1, 2:32]
        ncs[0, 2 * c + 1] = cs[2, 2:32]
        nps[0, 2 * c] = pso[1, 1:16]
        nps[0, 2 * c + 1] = pso[2, 1:16]
        if ch == 3:
            ncp[0, b] = cs[0, 2:32]
            npp[0, b] = pso[0, 1:16]
    return (y_prompt, y_sample, ncp, ncs, npp, nps)
```
